# Optimizing a Trainium2 kernel written in Bass

```python
import jax, jax.numpy as jnp
from jax import lax
import numpy as np

D_MODEL = 1024
BATCH = 4
SEQ = 4096
DEPTH = 2

GRID_W = 64
N_MEM = 256
HEAD_DIM = 64
NA_HEADS = 6
NA_KH = 8
NA_KW = 16
DIL_PAIRS = ((128, 1), (512, 4), (2048, 16))
DIL_HEADS_PER_GROUP = 2
DIL_HEADS = 6
MEM_HEADS = 4
D_A = 384
D_B = 384
D_B_OUT = 128
D_M = 256
N_BRANCH = 3
D_IN = 3 * D_A + 3 * D_B + D_M + N_BRANCH * D_MODEL
D_FF = 2816
CONV_W = 3
Q_BLOCK = 128
RMS_EPS = 1e-6
NEG_INF = -1e30

kernel_name = 'hybrid_na_dilated_mem_encoder'


def rms_norm(x, g):
    xf = x.astype(jnp.float32)
    y = xf * lax.rsqrt(jnp.mean(xf * xf, axis=-1, keepdims=True) + RMS_EPS)
    return (y * g.astype(jnp.float32)).astype(x.dtype)


def split_heads(t, n_heads):
    B, S, _ = t.shape
    return t.reshape(B, S, n_heads, HEAD_DIM).transpose(0, 2, 1, 3)


def alibi_slopes(n):
    return 2.0 ** (-8.0 * jnp.arange(1, n + 1, dtype=jnp.float32) / n)


def neighborhood_attention(q, k, v, rpb):
    B, H, S, d = q.shape
    rows = S // GRID_W
    kh = min(NA_KH, rows)
    kw = NA_KW
    scale = d ** -0.5
    qg = q.reshape(B, H, rows, GRID_W, d)
    kg = k.reshape(B, H, rows, GRID_W, d)
    vg = v.reshape(B, H, rows, GRID_W, d)
    cols = jnp.arange(GRID_W)
    col_start = jnp.clip(cols - kw // 2, 0, GRID_W - kw)
    col_idx = col_start[:, None] + jnp.arange(kw)[None, :]
    col_off = col_idx - cols[:, None] + (NA_KW - 1)
    rpb_c = rpb[:, :, col_off]

    def one_row(i):
        r0 = jnp.clip(i - kh // 2, 0, rows - kh)
        k_band = lax.dynamic_slice_in_dim(kg, r0, kh, axis=2)
        v_band = lax.dynamic_slice_in_dim(vg, r0, kh, axis=2)
        k_win = k_band[:, :, :, col_idx, :]
        v_win = v_band[:, :, :, col_idx, :]
        q_row = lax.dynamic_index_in_dim(qg, i, axis=2, keepdims=False)
        s = jnp.einsum('bhqd,bhrqcd->bhqrc', q_row, k_win).astype(jnp.float32) * scale
        row_off = r0 + jnp.arange(kh) - i + (NA_KH - 1)
        bias = jnp.take(rpb_c, row_off, axis=1).transpose(0, 2, 1, 3)
        s = s + bias.astype(jnp.float32)[None]
        p = jax.nn.softmax(s.reshape(B, H, GRID_W, kh * kw), axis=-1).reshape(B, H, GRID_W, kh, kw)
        return jnp.einsum('bhqrc,bhrqcd->bhqd', p.astype(v.dtype), v_win)

    out = lax.map(one_row, jnp.arange(rows))
    return out.transpose(1, 2, 0, 3, 4).reshape(B, H, S, d)


def dilated_attention(q, k, v):
    B, _, S, d = q.shape
    n_blk = S // Q_BLOCK
    scale = d ** -0.5
    slopes = alibi_slopes(DIL_HEADS)
    outs, lses = [], []
    for g, (window, dilation) in enumerate(DIL_PAIRS):
        hs = slice(g * DIL_HEADS_PER_GROUP, (g + 1) * DIL_HEADS_PER_GROUP)
        qg, kg, vg = q[:, hs], k[:, hs], v[:, hs]
        half = window // 2 // dilation
        offs = dilation * jnp.arange(-half, half + 1)
        alibi = -slopes[hs][:, None, None] * jnp.abs(offs).astype(jnp.float32)[None, None, :]
        qb = qg.reshape(B, DIL_HEADS_PER_GROUP, n_blk, Q_BLOCK, d)

        def one_block(bi):
            pos = bi * Q_BLOCK + jnp.arange(Q_BLOCK)
            idx = pos[:, None] + offs[None, :]
            valid = (idx >= 0) & (idx < S)
            idx_c = jnp.clip(idx, 0, S - 1)
            kk = jnp.take(kg, idx_c, axis=2)
            vv = jnp.take(vg, idx_c, axis=2)
            qq = lax.dynamic_index_in_dim(qb, bi, axis=2, keepdims=False)
            s = jnp.einsum('bhqd,bhqkd->bhqk', qq, kk).astype(jnp.float32) * scale + alibi[None]
            s = jnp.where(valid[None, None], s, NEG_INF)
            m = jnp.max(s, axis=-1, keepdims=True)
            p = jnp.exp(s - m)
            den = jnp.sum(p, axis=-1, keepdims=True)
            o = jnp.einsum('bhqk,bhqkd->bhqd', (p / den).astype(v.dtype), vv)
            lse = (m + jnp.log(den))[..., 0]
            return o, lse

        o, lse = lax.map(one_block, jnp.arange(n_blk))
        outs.append(o.transpose(1, 2, 0, 3, 4).reshape(B, DIL_HEADS_PER_GROUP, S, d))
        lses.append(lse.transpose(1, 2, 0, 3).reshape(B, DIL_HEADS_PER_GROUP, S))
    o_all = jnp.stack(outs, axis=0)
    alpha = jax.nn.softmax(jnp.stack(lses, axis=0), axis=0)
    out = jnp.einsum('gbhs,gbhsd->bshd', alpha.astype(o_all.dtype), o_all)
    return out.reshape(B, S, D_B_OUT)


def memory_attention(q, mem_n, w_kv):
    B, S, _ = q.shape
    kv = mem_n @ w_kv
    km, vm = jnp.split(kv, 2, axis=-1)
    qh = q.reshape(B, S, MEM_HEADS, HEAD_DIM)
    km = km.reshape(B, -1, MEM_HEADS, HEAD_DIM)
    vm = vm.reshape(B, -1, MEM_HEADS, HEAD_DIM)
    s = jnp.einsum('bshd,bmhd->bhsm', qh, km).astype(jnp.float32) * (HEAD_DIM ** -0.5)
    p = jax.nn.softmax(s, axis=-1)
    o = jnp.einsum('bhsm,bmhd->bshd', p.astype(vm.dtype), vm)
    return o.reshape(B, S, D_M)


def depthwise_conv_centred(u, w, b):
    S = u.shape[1]
    pad = CONV_W // 2
    up = jnp.pad(u, ((0, 0), (pad, pad), (0, 0)))
    y = b
    for j in range(CONV_W):
        y = y + up[:, j:j + S] * w[j]
    return y


def setup_inputs(seed: int = 0) -> dict:
    key = jax.random.key(seed)
    ks = jax.random.split(key, 20)
    f32 = jnp.float32

    def nrm(k, shape, scale):
        return jax.random.normal(k, shape, f32) * scale

    def gain(k, shape):
        return 1.0 + 0.05 * jax.random.normal(k, shape, f32)

    return {
        'x': nrm(ks[0], (BATCH, SEQ, D_MODEL), 1.0),
        'mem': nrm(ks[1], (BATCH, N_MEM, D_MODEL), 1.0),
        'mem_norm_g': gain(ks[2], (D_MODEL,)),
        'g_pre_mix': gain(ks[3], (DEPTH, D_MODEL)),
        'w_in': nrm(ks[4], (DEPTH, D_MODEL, D_IN), D_MODEL ** -0.5),
        'rpb_na': nrm(ks[5], (DEPTH, NA_HEADS, 2 * NA_KH - 1, 2 * NA_KW - 1), 0.5),
        'w_mem_kv': nrm(ks[6], (DEPTH, D_MODEL, 2 * D_M), D_MODEL ** -0.5),
        'b_gate': nrm(ks[7], (DEPTH, N_BRANCH, D_MODEL), 0.1),
        'w_br_a': nrm(ks[8], (DEPTH, D_A, D_MODEL), D_A ** -0.5),
        'w_br_b': nrm(ks[9], (DEPTH, D_B_OUT, D_MODEL), D_B_OUT ** -0.5),
        'w_br_m': nrm(ks[10], (DEPTH, D_M, D_MODEL), D_M ** -0.5),
        'w_out': nrm(ks[11], (DEPTH, D_MODEL, D_MODEL), D_MODEL ** -0.5),
        'g_post_mix': gain(ks[12], (DEPTH, D_MODEL)),
        'g_pre_ffn': gain(ks[13], (DEPTH, D_MODEL)),
        'w_up': nrm(ks[14], (DEPTH, D_MODEL, 2 * D_FF), D_MODEL ** -0.5),
        'conv_w': nrm(ks[15], (DEPTH, CONV_W, 2 * D_FF), CONV_W ** -0.5),
        'conv_b': nrm(ks[16], (DEPTH, 2 * D_FF), 0.02),
        'w_down': nrm(ks[17], (DEPTH, D_FF, D_MODEL), D_FF ** -0.5),
        'g_post_ffn': gain(ks[18], (DEPTH, D_MODEL)),
    }


def reference(x, mem, mem_norm_g, g_pre_mix, w_in, rpb_na, w_mem_kv, b_gate, w_br_a, w_br_b,
              w_br_m, w_out, g_post_mix, g_pre_ffn, w_up, conv_w, conv_b, w_down, g_post_ffn):
    B, S, D = x.shape
    splits = [D_A, 2 * D_A, 3 * D_A, 3 * D_A + D_B, 3 * D_A + 2 * D_B, 3 * D_A + 3 * D_B,
              3 * D_A + 3 * D_B + D_M]
    mem_n = rms_norm(mem, mem_norm_g)
    for l in range(DEPTH):
        h = rms_norm(x, g_pre_mix[l])
        proj = h @ w_in[l]
        q_a, k_a, v_a, q_b, k_b, v_b, q_m, gate_pre = jnp.split(proj, splits, axis=-1)
        o_a = neighborhood_attention(split_heads(q_a, NA_HEADS), split_heads(k_a, NA_HEADS),
                                     split_heads(v_a, NA_HEADS), rpb_na[l])
        o_a = o_a.transpose(0, 2, 1, 3).reshape(B, S, D_A)
        o_b = dilated_attention(split_heads(q_b, DIL_HEADS), split_heads(k_b, DIL_HEADS),
                                split_heads(v_b, DIL_HEADS))
        o_m = memory_attention(q_m, mem_n, w_mem_kv[l])
        gates = jax.nn.sigmoid(gate_pre.reshape(B, S, N_BRANCH, D) + b_gate[l])
        merged = (gates[:, :, 0] * (o_a @ w_br_a[l])
                  + gates[:, :, 1] * (o_b @ w_br_b[l])
                  + gates[:, :, 2] * (o_m @ w_br_m[l]))
        x = x + rms_norm(merged @ w_out[l], g_post_mix[l])
        h = rms_norm(x, g_pre_ffn[l])
        u = depthwise_conv_centred(h @ w_up[l], conv_w[l], conv_b[l])
        a, b = jnp.split(u, 2, axis=-1)
        f = jax.nn.gelu(a) * b
        x = x + rms_norm(f @ w_down[l], g_post_ffn[l])
    return x
```

```python
import numpy as np
import ml_dtypes
from contextlib import ExitStack
import concourse.bass as bass
import concourse.mybir as mybir
from concourse.bass_utils import run_bass_kernel_spmd

F32 = mybir.dt.float32
BF16 = mybir.dt.bfloat16
AF = mybir.ActivationFunctionType
ALU = mybir.AluOpType

D = 1024
SEQ = 4096
NMEM = 256
DIN = 5632
DFF = 2816
NEG = -30000.0
N_CORES = 8
OWN = 2048
TFL = (3200, 2176)
T = SEQ
NB = T // 512
NT = T // 128
NCV = 8 + 2 * 232
DIL = (1, 4, 16)


class _Stop(Exception):
    pass


class Sched:
    def __init__(self, nc, stack):
        self.nc = nc
        self.stack = stack
        self.eng = {"pe": nc.tensor, "act": nc.scalar, "dve": nc.vector, "pool": nc.gpsimd, "sp": nc.sync}
        self.sem = {}
        self.val = {}
        self.seen = {e: {} for e in self.eng}
        self.lw = {}
        self.rd = {}
        self.pending = {e: False for e in self.eng}
        self.halt = False
        for e in self.eng:
            self.sem[e] = stack.enter_context(nc.semaphore("sem_" + e))
            self.val[e] = 0

    def _stream(self, name):
        if name not in self.sem:
            self.sem[name] = self.stack.enter_context(self.nc.semaphore("dq_" + name))
            self.val[name] = 0
        return name

    def _deps(self, eng, R, W):
        deps = {}

        def add(p):
            k, v = p
            if k not in self.eng:
                v = self.val[k]
            if deps.get(k, 0) < v:
                deps[k] = v
        for k in R:
            if k in self.lw:
                add(self.lw[k])
        for k in W:
            if k in self.lw:
                add(self.lw[k])
            for p in self.rd.get(k, {}).items():
                add(p)
        h = self.eng[eng]
        for k, v in deps.items():
            if k == eng and eng == "pe":
                continue
            if self.seen[eng].get(k, 0) < v:
                h.wait_ge(self.sem[k], v)
                self.seen[eng][k] = v

    def _mark(self, me, R, W):
        for k in R:
            d = self.rd.setdefault(k, {})
            if d.get(me[0], 0) < me[1]:
                d[me[0]] = me[1]
        for k in W:
            self.lw[k] = me
            self.rd[k] = {}

    def op(self, eng, fn, R=(), W=(), inc=True):
        if self.halt:
            return
        self._deps(eng, R, W)
        ins = fn(self.eng[eng])
        if inc:
            ins.then_inc(self.sem[eng], 1)
            self.val[eng] += 1
            me = (eng, self.val[eng])
            self.pending[eng] = False
        else:
            me = (eng, self.val[eng] + 1)
            self.pending[eng] = True
        self._mark(me, R, W)

    def dma(self, q, stream, out, in_, R=(), W=()):
        if self.halt:
            return
        s = self._stream(stream)
        self._deps(q, R, W)
        self.eng[q].dma_start(out=out, in_=in_).then_inc(self.sem[s], 16)
        self.val[s] += 16
        self._mark((s, self.val[s]), R, W)

    def barrier(self):
        if self.halt:
            return
        for e in self.eng:
            assert not self.pending[e], e
        for e, h in self.eng.items():
            for k, v in self.val.items():
                if k == e:
                    continue
                if v > 0 and self.seen[e].get(k, 0) < v:
                    h.wait_ge(self.sem[k], v)
                    self.seen[e][k] = v
        self.lw = {}
        self.rd = {}


def build_nc(dbg=None, stop=None):
    nc = bass.Bass("TRN2", target_bir_lowering=False)

    def din(name, shape, dt=F32):
        return nc.dram_tensor(name, list(shape), dt, kind="ExternalInput").ap()

    def dscr(name, shape, dt):
        kind = "ExternalOutput" if (dbg and name in dbg) else "Internal"
        return nc.dram_tensor(name, list(shape), dt, kind=kind).ap()

    xT_d = din("xT", [NB, 128, 8, 512])
    memT_d = din("memT", [D, NMEM])
    cv_d = din("cv", [128, NCV])
    dilb_d = din("dilb", [128, 6, 384])
    ident_d = din("ident", [128, 128])
    nab_d = din("nab", [2, 5, 6, 128, 640])
    w_in_d = din("w_in", [2, D, DIN])
    w_kv_d = din("w_mem_kv", [2, D, 512])
    w_bra_d = din("w_br_a", [2, 384, D])
    w_brb_d = din("w_br_b", [2, 128, D])
    w_brm_d = din("w_br_m", [2, 256, D])
    w_out_d = din("w_out", [2, D, D])
    w_up_d = din("w_up", [2, D, DIN])
    w_dn_d = din("w_down", [2, DFF, D])
    out_d = nc.dram_tensor("outT", [OWN // 512, 128, 8, 512], F32, kind="ExternalOutput").ap()

    xa_d = dscr("xa", [NB, 128, 8, 512], F32)
    xb_d = dscr("xb", [NB, 128, 8, 512], F32)
    hT_d = dscr("hTd", [NB, 128, 8, 512], BF16)
    qk_d = dscr("qk", [14, 128, T], BF16)
    vp_d = dscr("vpad", [6, T, 256], BF16)
    o_d = dscr("oT", [128, 6, T], BF16)
    f_d = dscr("fT", [128, 22, T], BF16)

    def xview(ap, t0, sz=None):
        if sz is None:
            t0, sz = t0 * 512, 512
        assert t0 // 512 == (t0 + sz - 1) // 512
        return ap[t0 // 512, :, :, t0 % 512:t0 % 512 + sz]

    def blocks_of(tf):
        b = [(n * 512, 512) for n in range(tf // 512)]
        if tf % 512:
            b.append((tf - tf % 512, tf % 512))
        return b

    with ExitStack() as top:
        S = Sched(nc, top)

        uid = [0]

        def sb(stack, name, shape, dt):
            uid[0] += 1
            return stack.enter_context(nc.sbuf_tensor(f"s{uid[0]}_{name}", list(shape), dt))

        def ps(stack, name, shape, dt=F32):
            uid[0] += 1
            return stack.enter_context(nc.psum_tensor(f"p{uid[0]}_{name}", list(shape), dt))

        ones = sb(top, "ones", [128, 128], BF16)
        onesh = sb(top, "onesh", [128, 2, 128], BF16)
        eps = sb(top, "eps", [128, 1], F32)
        cv = sb(top, "cv", [128, NCV], F32)
        dilbb = sb(top, "dilbb", [128, 6, 384], BF16)
        ident = sb(top, "ident", [128, 128], BF16)
        mnT = sb(top, "mnT", [128, 8, NMEM], BF16)
        kmT_l = [sb(top, f"kmT{l}", [128, 2, NMEM], BF16) for l in range(2)]
        vmp_l = [sb(top, f"vmp{l}", [128, 2, 2, 256], BF16) for l in range(2)]
        S.op("dve", lambda e: e.memset(ones[:], 1.0), W=["ones"])
        S.op("dve", lambda e: e.memset(onesh[:], 0.0), W=["onesh"])
        S.op("dve", lambda e: e.memset(onesh[:, 0, 0:64], 1.0), W=["onesh"])
        S.op("dve", lambda e: e.memset(onesh[:, 1, 64:128], 1.0), W=["onesh"])
        S.op("dve", lambda e: e.memset(eps[:], 1e-6), W=["eps"])
        S.dma("sp", "const", cv[:], cv_d[:, :], W=["cv"])
        S.dma("pool", "constc", dilbb[:], dilb_d[:, :, :], W=["dilbb"])
        S.dma("pool", "constc", ident[:], ident_d[:, :], W=["ident"])
        n0_, o0_ = TFL[0] // 512, TFL[0] % 512
        if o0_:
            S.dma("sp", "const", xb_d[n0_, :, :, o0_:512], xT_d[n0_, :, :, o0_:512], W=["xbseed"])
            n0_ += 1
        for n_ in range(n0_, NB):
            S.dma("sp", "const", xb_d[n_], xT_d[n_], W=["xbseed"])
        S.barrier()

        def norm_p1(xb_ap, xkey, N, nb, i):
            k = i % len(nb["sq"])
            sq, pss, sd = nb["sq"][k], nb["pss"][k], nb["sd"][k]
            sqkey, psskey, sdkey = f"nsq{k}", f"npss{k}", f"nsd{k}"
            S.op("act", lambda e: e.activation(out=sq[:, :, 0:N], in_=xb_ap, func=AF.Square), R=[xkey], W=[sqkey])
            for c in range(8):
                S.op("pe", lambda e, c=c: e.matmul(pss[:, 0:N], lhsT=ones[:], rhs=sq[:, c, 0:N], start=(c == 0), stop=(c == 7)),
                     R=[sqkey, "ones"], W=[psskey], inc=(c == 7))
            S.op("act", lambda e: e.activation(out=sd[:, 0:N], in_=pss[:, 0:N], func=AF.Sqrt, scale=1.0 / D, bias=eps[:]),
                 R=[psskey, "eps"], W=[sdkey])

        def norm_p1b(N, nb, i):
            k = i % len(nb["sq"])
            sd = nb["sd"][k]
            S.op("dve", lambda e: e.reciprocal(out=sd[:, 0:N], in_=sd[:, 0:N]), R=[f"nsd{k}"], W=[f"nsd{k}"])

        def norm_p2(xb_ap, xkey, N, gcol, h_ap, hkey, nb, i):
            k = i % len(nb["sq"])
            sd = nb["sd"][k]
            tp = nb["tp"][k] if nb.get("tp") else None
            sdkey, tpkey = f"nsd{k}", f"ntp{k}"
            npool = 2 if tp is not None else 0
            for c in range(8 - npool, 8):
                cc = c - (8 - npool)
                S.op("pool", lambda e, c=c, cc=cc: e.tensor_tensor(out=tp[:, cc, 0:N], in0=xb_ap[:, c, :], in1=sd[:, 0:N], op=ALU.mult),
                     R=[xkey, sdkey], W=[tpkey])
            for c in range(8 - npool):
                S.op("dve", lambda e, c=c: e.scalar_tensor_tensor(out=h_ap[:, c, :], in0=xb_ap[:, c, :], scalar=cv[:, gcol + c:gcol + c + 1],
                                                                   in1=sd[:, 0:N], op0=ALU.mult, op1=ALU.mult),
                     R=[xkey, sdkey, "cv"], W=[hkey])
            for c in range(8 - npool, 8):
                cc = c - (8 - npool)
                S.op("act", lambda e, c=c, cc=cc: e.activation(out=h_ap[:, c, :], in_=tp[:, cc, 0:N], func=AF.Copy, scale=cv[:, gcol + c:gcol + c + 1]),
                     R=[tpkey, "cv"], W=[hkey])

        def norm_block(xb_ap, xkey, N, gcol, h_ap, hkey, nb, i):
            norm_p1(xb_ap, xkey, N, nb, i)
            norm_p1b(N, nb, i)
            norm_p2(xb_ap, xkey, N, gcol, h_ap, hkey, nb, i)

        with ExitStack() as st:
            mx = sb(st, "mx", [128, 8, NMEM], F32)
            msq = sb(st, "msq", [128, 8, NMEM], BF16)
            msd = sb(st, "msd", [128, NMEM], F32)
            mps = ps(st, "mps", [128, 512])
            S.dma("sp", "ld0", mx[:], memT_d.rearrange("(c p) t -> p c t", p=128), W=["mx"])
            norm_block(mx[:], "mx", NMEM, 0, mnT[:], "mnT", dict(sq=[msq], pss=[mps], sd=[msd]), 0)
            S.barrier()

        def on(l, ph):
            return stop is None or (l, ph) <= stop

        def chk(l, tag):
            if stop == (l, tag):
                S.barrier()
                S.halt = True

        try:
            for l in range(2):
                cb = 8 + l * 232
                if not on(l, "A"):
                    break
                xsrc = xT_d if l == 0 else xb_d
                xmid = xa_d
                xdst = xb_d if l == 0 else out_d

                with ExitStack() as st:
                    hT = sb(st, "hT", [128, 8, T], BF16)
                    pp = [ps(st, f"pp{i}", [128, 512]) for i in range(2)]
                    pv = [ps(st, f"pv{i}", [128, 512]) for i in range(2)]
                    wbuf = [sb(st, f"wbuf{i}", [128, 8, 512], BF16) for i in range(2)]
                    wv = sb(st, "wv", [128, 8, 768], BF16)
                    wkv = sb(st, "wkv", [128, 8, 512], BF16)
                    kmT, vmp = kmT_l[l], vmp_l[l]
                    NVST = 8

                    win = w_in_d[l].rearrange("(c p) n -> p c n", p=128)
                    S.dma("pool", "wv", wv[:, :, 0:384], win[:, :, 768:1152], W=["wv"])
                    S.dma("pool", "wv", wv[:, :, 384:768], win[:, :, 1920:2304], W=["wv"])
                    S.dma("pool", "wkv", wkv[:], w_kv_d[l].rearrange("(c p) n -> p c n", p=128), W=["wkv"])
                    groups = [(0, 512, [0, 1, 2, 3]), (512, 256, [4, 5]), (1152, 512, [6, 7, 8, 9]),
                              (1664, 256, [10, 11]), (2304, 256, [12, 13])]

                    def load_group(gi):
                        c0, w, _ = groups[gi]
                        S.dma("pool", f"wb{gi % 2}", wbuf[gi % 2][:, :, 0:w], win[:, :, c0:c0 + w], W=[f"wbuf{gi % 2}"])
                    load_group(0)
                    load_group(1)

                    with ExitStack() as st2:
                        xbuf = [sb(st2, f"xbuf{i}", [128, 8, 512], F32) for i in range(3)]
                        nbufs = dict(sq=[sb(st2, f"sq{i}", [128, 8, 512], BF16) for i in range(2)],
                                     sd=[sb(st2, f"sd{i}", [128, 512], F32) for i in range(2)],
                                     tp=[sb(st2, f"tp{i}", [128, 3, 512], F32) for i in range(2)],
                                     pss=[ps(st2, f"pss{i}", [128, 512]) for i in range(2)])
                        S.dma("sp", "ldx0", xbuf[0][:], xview(xsrc, 0), W=["xbuf0"])
                        for n in range(NB + 1):
                            if n + 1 < NB:
                                S.dma("sp", f"ldx{(n + 1) % 3}", xbuf[(n + 1) % 3][:], xview(xsrc, n + 1), W=[f"xbuf{(n + 1) % 3}"])
                            if n < NB:
                                norm_p1(xbuf[n % 3][:], f"xbuf{n % 3}", 512, nbufs, n)
                            if n >= 1:
                                m_ = n - 1
                                norm_p2(xbuf[m_ % 3][:], f"xbuf{m_ % 3}", 512, cb + 0, hT[:, :, m_ * 512:(m_ + 1) * 512], f"hT:{m_}", nbufs, m_)
                                S.dma("pool", "sthT", hT_d[m_], hT[:, :, m_ * 512:(m_ + 1) * 512], R=[f"hT:{m_}"], W=[f"hTd:{m_}"])
                            if n < NB:
                                norm_p1b(512, nbufs, n)
                        S.barrier()
                    stage = [sb(st, f"stage{i}", [128, T], BF16) for i in range(2)]
                    vst = [sb(st, f"vst{i}", [128, 3, 256], BF16) for i in range(NVST)]
                    for i in range(NVST):
                        S.op("pool", lambda e, i=i: e.memset(vst[i][:], 0.0), W=[f"vst{i}"])
                    hkeys = [f"hT:{n}" for n in range(NB)]
                    chk(l, "A1")

                    TFc = TFL[l]
                    nbq = -(-TFc // 512)
                    kmaxs = []
                    for g_ in range(3):
                        d_ = DIL[g_]
                        ntc_ = NT // d_
                        kmaxs.append(min(ntc_ - 1, min(ntc_, -(-TFc // (128 * d_)))))
                    nb_of = {}
                    for idx_ in (0, 1, 2, 6, 7, 8, 12, 13):
                        nb_of[idx_] = nbq
                    for idx_ in (3, 4, 5):
                        nb_of[idx_] = min(NB, -(-(TFc + 256) // 512))
                    for g_ in range(3):
                        nb_of[9 + g_] = min(NB, -(-(DIL[g_] * 128 * (kmaxs[g_] + 1)) // 512))
                    nva = min(NT, TFc // 128 + 2)
                    cnt = 0
                    ci = 0
                    for gi, (c0, w, idxs) in enumerate(groups):
                        wb = wbuf[gi % 2]
                        for jj, idx in enumerate(idxs):
                            sg = stage[ci % 2]
                            for n in range(nb_of[idx]):
                                p = pp[cnt % 2]
                                for c in range(8):
                                    S.op("pe", lambda e, c=c, p=p, jj=jj, n=n, wb=wb: e.matmul(p[:], lhsT=wb[:, c, jj * 128:(jj + 1) * 128],
                                                                                                  rhs=hT[:, c, n * 512:(n + 1) * 512], start=(c == 0), stop=(c == 7)),
                                         R=[f"wbuf{gi % 2}", f"hT:{n}"], W=[f"pp{cnt % 2}"], inc=(c == 7))
                                qs = 0.125 if idx in (0, 1, 2, 6, 7, 8, 12, 13) else 1.0
                                if cnt % 2 == 0:
                                    S.op("act", lambda e, p=p, sg=sg, n=n: e.activation(out=sg[:, n * 512:(n + 1) * 512], in_=p[:], func=AF.Copy, scale=qs),
                                         R=[f"pp{cnt % 2}"], W=[f"stage{ci % 2}"])
                                else:
                                    S.op("dve", lambda e, p=p, sg=sg, n=n: e.tensor_scalar(out=sg[:, n * 512:(n + 1) * 512], in0=p[:], scalar1=qs, scalar2=None,
                                                                                             op0=ALU.mult),
                                         R=[f"pp{cnt % 2}"], W=[f"stage{ci % 2}"])
                                cnt += 1
                            S.dma("pool", "stqk", qk_d[idx, :, 0:nb_of[idx] * 512], sg[:, 0:nb_of[idx] * 512], R=[f"stage{ci % 2}"], W=[f"qk:{idx}"])
                            ci += 1
                        if gi + 2 < len(groups):
                            load_group(gi + 2)

                    chk(l, "A2")
                    vc = 0
                    for t in range(nva):
                        p = pv[vc % 2]
                        v = vst[vc % NVST]
                        for c in range(8):
                            S.op("pe", lambda e, c=c, p=p, t=t: e.matmul(p[:, 0:384], lhsT=hT[:, c, t * 128:(t + 1) * 128], rhs=wv[:, c, 0:384],
                                                                          start=(c == 0), stop=(c == 7)),
                                 R=["wv", f"hT:{t // 4}"], W=[f"pv{vc % 2}"], inc=(c == 7))
                        dst = v[:].rearrange("p a (b d) -> p a b d", d=64)[:, :, 0:4:3, :]
                        src = p[:, 0:384].rearrange("p (a h d) -> p a h d", a=3, h=2)
                        S.op("dve" if vc % 2 else "act",
                             (lambda e, dst=dst, src=src: e.tensor_copy(out=dst, in_=src)) if vc % 2 else
                             (lambda e, dst=dst, src=src: e.activation(out=dst, in_=src, func=AF.Copy)),
                             R=[f"pv{vc % 2}"], W=[f"vst{vc % NVST}"])
                        S.dma("pool" if vc % 2 else "sp", f"stv{vc % NVST}", vp_d[0:3, t * 128:(t + 1) * 128, :].rearrange("a p x -> p a x"), v[:], R=[f"vst{vc % NVST}"], W=[f"vpd{vc}"])
                        vc += 1
                    for g in range(3):
                        d = DIL[g]
                        ntc = NT // d
                        for r in range(d):
                            for j in range(kmaxs[g] + 1):
                                p = pv[vc % 2]
                                v = vst[vc % NVST]
                                t0 = r + d * 128 * j
                                for c in range(8):
                                    S.op("pe", lambda e, c=c, p=p, t0=t0, d=d, g=g: e.matmul(p[:, 0:128], lhsT=hT[:, c, t0:t0 + d * 127 + 1:d],
                                                                                             rhs=wv[:, c, 384 + g * 128:384 + (g + 1) * 128], start=(c == 0), stop=(c == 7)),
                                         R=["wv"] + hkeys, W=[f"pv{vc % 2}"], inc=(c == 7))
                                dst = v[:, 0, :].rearrange("p (b d) -> p b d", d=64)[:, 0:4:3, :]
                                src = p[:, 0:128].rearrange("p (h d) -> p h d", h=2)
                                S.op("dve" if vc % 2 else "act",
                                     (lambda e, dst=dst, src=src: e.tensor_copy(out=dst, in_=src)) if vc % 2 else
                                     (lambda e, dst=dst, src=src: e.activation(out=dst, in_=src, func=AF.Copy)),
                                     R=[f"pv{vc % 2}"], W=[f"vst{vc % NVST}"])
                                row0 = (r * ntc + j) * 128
                                S.dma("pool" if vc % 2 else "sp", f"stv{vc % NVST}", vp_d[3 + g, row0:row0 + 128, :], v[:, 0, :], R=[f"vst{vc % NVST}"], W=[f"vpd{vc}"])
                                vc += 1

                    chk(l, "A3")
                    S.op("pool", lambda e: e.memset(vmp[:], 0.0), W=["vmp"])
                    for pr in range(2):
                        p = pp[pr]
                        for c in range(8):
                            S.op("pe", lambda e, c=c, p=p, pr=pr: e.matmul(p[:, 0:NMEM], lhsT=wkv[:, c, pr * 128:(pr + 1) * 128], rhs=mnT[:, c, :],
                                                                            start=(c == 0), stop=(c == 7)),
                                 R=["wkv", "mnT"], W=[f"pp{pr}"], inc=(c == 7))
                        S.op("act", lambda e, p=p, pr=pr: e.activation(out=kmT[:, pr, :], in_=p[:, 0:NMEM], func=AF.Copy), R=[f"pp{pr}"], W=["kmT"])
                    for kt in range(2):
                        p = pv[kt]
                        for c in range(8):
                            S.op("pe", lambda e, c=c, p=p, kt=kt: e.matmul(p[:, 0:256], lhsT=mnT[:, c, kt * 128:(kt + 1) * 128], rhs=wkv[:, c, 256:512],
                                                                            start=(c == 0), stop=(c == 7)),
                                 R=["wkv", "mnT"], W=[f"pv{kt}"], inc=(c == 7))
                        dst = vmp[:, kt, :, :].rearrange("p a (b d) -> p a b d", d=64)[:, :, 0:4:3, :]
                        src = p[:, 0:256].rearrange("p (a h d) -> p a h d", a=2, h=2)
                        S.op("dve", lambda e, dst=dst, src=src: e.tensor_copy(out=dst, in_=src), R=[f"pv{kt}"], W=["vmp"])
                    S.barrier()

                if not on(l, "C"):
                    break
                with ExitStack() as st:
                    qz = [sb(st, f"qz{i}", [128, 2, T], BF16) for i in range(2)]
                    kT = [sb(st, f"kT{i}", [128, T], BF16) for i in range(2)]
                    vp = [sb(st, f"vp{i}", [128, NT, 256], BF16) for i in range(2)]
                    for i in range(2):
                        S.op("pool", lambda e, i=i: e.memset(qz[i][:], 0.0), W=[f"qT{i}"])
                    nabb = [sb(st, f"nabb{i}", [128, 3, 2, 640], BF16) for i in range(2)]
                    numB = sb(st, "numB", [128, T], F32)
                    denB = sb(st, "denB", [128, T], F32)
                    ost = [sb(st, f"ost{i}", [128, T], BF16) for i in range(2)]
                    pT = [sb(st, f"pT{i}", [128, 1536], BF16) for i in range(2)]
                    rec = [sb(st, f"rec{i}", [128, 256], F32) for i in range(2)]
                    pS = [ps(st, f"pS{i}", [128, 1536]) for i in range(2)]
                    pOD = [ps(st, f"pOD{i}", [128, 512]) for i in range(2)]

                    jobs = [("A", 0, 3, 0), ("A", 1, 4, 1), ("A", 2, 5, 2), ("B", 6, 9, 3), ("B", 7, 10, 4), ("B", 8, 11, 5),
                            ("M", 12, None, None), ("M", 13, None, None)]

                    def load_job(ji):
                        kind, qi, ki, vi = jobs[ji]
                        s = ji % 2
                        S.dma("sp", f"ldq{s}", qz[s][0:64, 0, :], qk_d[qi, 0:64, :], W=[f"qT{s}"])
                        S.dma("sp", f"ldq{s}", qz[s][64:128, 1, :], qk_d[qi, 64:128, :], W=[f"qT{s}"])
                        if ki is not None:
                            S.dma("sp", f"ldq{s}", kT[s][:], qk_d[ki], W=[f"kT{s}"])
                            S.dma("sp", f"ldq{s}", vp[s][:], vp_d[vi].rearrange("(t p) x -> p t x", p=128), W=[f"vp{s}"])
                        if kind == "A":
                            for v_ in range(3):
                                S.dma("pool", f"ldnab{s}", nabb[s][:, v_, :, :], nab_d[l, v_, 2 * ji:2 * ji + 2, :, :].rearrange("h p x -> p h x"),
                                      W=[f"nabb{s}"])

                    MM = dict(skip_group_check=True)
                    tiles = []

                    def add_tile(ji, units, bias_pieces, nd_list, ncols, out_cb, first=False):
                        st_ = {}

                        def stage0():
                            if first and ji + 1 < len(jobs):
                                load_job(ji + 1)
                            b = st_["b"] = hc[0] % 2
                            hc[0] += 1
                            started = set()

                            def flag(c0):
                                bk = c0 // 512
                                f = bk not in started
                                started.add(bk)
                                return f
                            pieces = []
                            for (c0, n, ap, key) in bias_pieces:
                                o = 0
                                while o < n:
                                    m = min(n - o, 512 - (c0 + o) % 512)
                                    pieces.append((c0 + o, m, ap[:, o:o + m], key))
                                    o += m
                            for (c0, n, ap, key) in pieces:
                                S.op("pe", lambda e: e.matmul(pS[b][:, c0:c0 + n], lhsT=ident[:], rhs=ap, start=flag(c0), stop=False, **MM),
                                     R=["ident", key], W=[f"pS{b}"], inc=False)
                            for ui, (c0, n, la, lk, ra, rk) in enumerate(units):
                                S.op("pe", lambda e: e.matmul(pS[b][:, c0:c0 + n], lhsT=la, rhs=ra, start=flag(c0), stop=True, **MM),
                                     R=[lk, rk], W=[f"pS{b}"], inc=(ui == len(units) - 1))
                            if _os.environ.get("ATT_EXP", "1") == "1":
                                S.op("act", lambda e: e.activation(out=pT[b][:, 0:ncols], in_=pS[b][:, 0:ncols], func=AF.Exp),
                                     R=[f"pS{b}"], W=[f"pT{b}"])

                        def stage1():
                            b = st_["b"]
                            ob = tc[0] % 2
                            tc[0] += 1
                            w = nd_list[0][1]
                            for ni, (c0, n, va, vk, hh) in enumerate(nd_list):
                                S.op("pe", lambda e: e.matmul(pOD[ob][:, 0:w], lhsT=va, rhs=pT[b][:, c0:c0 + n], start=(ni == 0), stop=False, **MM),
                                     R=[vk, f"pT{b}"], W=[f"pOD{ob}"], inc=False)
                            for ni, (c0, n, va, vk, hh) in enumerate(nd_list):
                                S.op("pe", lambda e: e.matmul(pOD[ob][:, w:2 * w], lhsT=onesh[:, hh, :], rhs=pT[b][:, c0:c0 + n], start=False,
                                                              stop=(ni == len(nd_list) - 1), **MM),
                                     R=["onesh", f"pT{b}"], W=[f"pOD{ob}"], inc=(ni == len(nd_list) - 1))
                            out_cb(ob, w)
                        tiles.append((stage0, stage1))

                    hc = [0]
                    tc = [0]

                    def out_norm(osb, qsl):
                        def cbk(ob, w):
                            r = rec[ob]
                            S.op("dve", lambda e: e.reciprocal(out=r[:, 0:w], in_=pOD[ob][:, w:2 * w]), R=[f"pOD{ob}"], W=[f"rec{ob}"])
                            S.op("dve", lambda e: e.tensor_tensor(out=ost[osb][:, qsl], in0=pOD[ob][:, 0:w], in1=r[:, 0:w], op=ALU.mult),
                                 R=[f"pOD{ob}", f"rec{ob}"], W=[f"ost{osb}"])
                        return cbk

                    def out_acc(first, qsl):
                        def cbk(ob, w):
                            if first:
                                S.op("dve", lambda e: e.tensor_copy(out=numB[:, qsl], in_=pOD[ob][:, 0:w]), R=[f"pOD{ob}"], W=["numB"])
                                S.op("dve", lambda e: e.tensor_copy(out=denB[:, qsl], in_=pOD[ob][:, w:2 * w]), R=[f"pOD{ob}"], W=["denB"])
                            else:
                                S.op("dve", lambda e: e.tensor_tensor(out=numB[:, qsl], in0=pOD[ob][:, 0:w], in1=numB[:, qsl], op=ALU.add),
                                     R=[f"pOD{ob}", "numB"], W=["numB"])
                                S.op("dve", lambda e: e.tensor_tensor(out=denB[:, qsl], in0=pOD[ob][:, w:2 * w], in1=denB[:, qsl], op=ALU.add),
                                     R=[f"pOD{ob}", "denB"], W=["denB"])
                        return cbk

                    def with_tail(cbk, tail):
                        def f(ob, w):
                            cbk(ob, w)
                            tail()
                        return f

                    def std_tile(ji, s, ksl, kTt, qsl, bias_pieces, vfn, out_cb, first):
                        nk = len(ksl)
                        W_ = nk * 128
                        units = []
                        nd = []
                        for hh in range(2):
                            rows = slice(hh * 64, (hh + 1) * 64)
                            for jj, ks in enumerate(ksl):
                                units.append((hh * W_ + jj * 128, 128, kTt[0][:, ks], kTt[1], qz[s][:, hh, qsl], f"qT{s}"))
                                va = vfn(jj, hh)
                                nd.append((hh * W_ + jj * 128, 128, va[0], va[1], hh))
                        add_tile(ji, units, bias_pieces, nd, 2 * W_, out_cb, first)

                    load_job(0)
                    oc = 0
                    import os as _os
                    _only = _os.environ.get("ATT_ONLY", "ABM")
                    for ji, (kind, qi, ki, vi) in enumerate(jobs):
                        s = ji % 2
                        if kind not in _only:
                            continue
                        if kind == "A":
                            pr = ji
                            osb = oc % 2
                            oc += 1
                            nm_ = TFL[l] // 128
                            for m in range(nm_):
                                kt0 = min(max(m - 2, 0), NT - 5)
                                var = {0: 0, 1: 1}.get(m, 2)
                                ksl = [slice((kt0 + j) * 128, (kt0 + j + 1) * 128) for j in range(5)]
                                qsl = slice(m * 128, (m + 1) * 128)
                                bp = [(0, 1280, nabb[s][:, var, :, :].rearrange("p h x -> p (h x)"), f"nabb{s}")]
                                cbk = out_norm(osb, qsl)
                                if m == nm_ - 1:
                                    cbk = with_tail(cbk, lambda pr=pr, osb=osb: S.dma("pool", "sto", o_d[:, pr, :], ost[osb][:], R=[f"ost{osb}"], W=[f"od:{pr}"]))
                                std_tile(ji, s, ksl, (kT[s], f"kT{s}"), qsl, bp,
                                         lambda jj, hh, kt0=kt0, s=s: (vp[s][:, kt0 + jj, hh * 128:(hh + 1) * 128], f"vp{s}"), cbk, m == 1)
                        elif kind == "B":
                            g = ji - 3
                            d = DIL[g]
                            ntc = NT // d
                            njq = min(ntc, -(-TFL[l] // (128 * d)))
                            for r in range(d):
                                for j in range(njq):
                                    jl = [jj for jj in (j - 1, j, j + 1) if 0 <= jj < ntc]
                                    ksl = [slice(r + d * 128 * jj, r + d * 128 * jj + d * 127 + 1, d) for jj in jl]
                                    qsl = slice(r + d * 128 * j, r + d * 128 * j + d * 127 + 1, d)
                                    b0 = (jl[0] - (j - 1)) * 128
                                    W_ = len(jl) * 128
                                    if W_ == 384:
                                        bp = [(0, 768, dilbb[:, 2 * g:2 * g + 2, :].rearrange("p h x -> p (h x)"), "dilbb")]
                                    else:
                                        bp = [(hh * W_, W_, dilbb[:, 2 * g + hh, b0:b0 + W_], "dilbb") for hh in range(2)]
                                    cbk = out_acc(g == 0, qsl)
                                    if g == 2 and r == d - 1 and j == njq - 1:
                                        osb = oc % 2
                                        oc += 1

                                        def tailB(osb=osb):
                                            S.op("dve", lambda e: e.reciprocal(out=denB[:], in_=denB[:]), R=["denB"], W=["denB"])
                                            S.op("dve", lambda e: e.tensor_tensor(out=ost[osb][:], in0=numB[:], in1=denB[:], op=ALU.mult),
                                                 R=["numB", "denB"], W=[f"ost{osb}"])
                                            S.dma("pool", "sto", o_d[:, 3, :], ost[osb][:], R=[f"ost{osb}"], W=["od:3"])
                                        cbk = with_tail(cbk, tailB)
                                    std_tile(ji, s, ksl, (kT[s], f"kT{s}"), qsl, bp,
                                             lambda jj, hh, jl=jl, r=r, ntc=ntc, s=s: (vp[s][:, r * ntc + jl[jj], hh * 128:(hh + 1) * 128], f"vp{s}"),
                                             cbk, r == 0 and j == 1)
                        else:
                            pr = ji - 6
                            osb = oc % 2
                            oc += 1
                            qb = [(q0, min(256, TFL[l] - q0)) for q0 in range(0, TFL[l], 256)]
                            for bi, (q0, w) in enumerate(qb):
                                qsl = slice(q0, q0 + w)
                                units = []
                                nd = []
                                for hh in range(2):
                                    rows = slice(hh * 64, (hh + 1) * 64)
                                    for kt in range(2):
                                        c0 = (hh * 2 + kt) * w
                                        units.append((c0, w, kmT[:, pr, kt * 128:(kt + 1) * 128], "kmT", qz[s][:, hh, qsl], f"qT{s}"))
                                        nd.append((c0, w, vmp[:, kt, pr, hh * 128:(hh + 1) * 128], "vmp", hh))
                                cbk = out_norm(osb, qsl)
                                if bi == len(qb) - 1:
                                    cbk = with_tail(cbk, lambda pr=pr, osb=osb: S.dma("pool", "sto", o_d[:, 4 + pr, :], ost[osb][:], R=[f"ost{osb}"], W=[f"od:{4 + pr}"]))
                                add_tile(ji, units, [], nd, 4 * w, cbk, bi == 1)
                    for g_ in range(len(tiles) + 1):
                        if g_ < len(tiles):
                            tiles[g_][0]()
                        if g_ >= 1 and _os.environ.get("ATT_ST1", "1") == "1":
                            tiles[g_ - 1][1]()
                    S.barrier()

                def post_head(st_bufs, emit_y, sz):
                    yT, ysq, sd2, pY, pSS = st_bufs
                    for j2 in range(8):
                        p = pY[j2 % 2]
                        emit_y(j2, p, f"pY{j2 % 2}")
                        S.op("act", lambda e, p=p, j2=j2: e.activation(out=yT[:, j2, 0:sz], in_=p[:, 0:sz], func=AF.Copy), R=[f"pY{j2 % 2}"], W=[f"yT:{j2}"])
                        S.op("act", lambda e, p=p, j2=j2: e.activation(out=ysq[:, j2, 0:sz], in_=p[:, 0:sz], func=AF.Square), R=[f"pY{j2 % 2}"], W=[f"ysq:{j2}"])
                    for c in range(8):
                        S.op("pe", lambda e, c=c: e.matmul(pSS[:, 0:sz], lhsT=ones[:], rhs=ysq[:, c, 0:sz], start=(c == 0), stop=(c == 7)),
                             R=[f"ysq:{c}", "ones"], W=["pSS"], inc=(c == 7))
                    S.op("act", lambda e: e.activation(out=sd2[:, 0:sz], in_=pSS[:, 0:sz], func=AF.Sqrt, scale=1.0 / D, bias=eps[:]), R=["pSS", "eps"], W=["sd2"])

                def post_tail(st_bufs, sz, gcol, xb_t, xkey, dst_ap):
                    yT, ysq, sd2, pY, pSS = st_bufs
                    pieces = [lambda: S.op("dve", lambda e: e.reciprocal(out=sd2[:, 0:sz], in_=sd2[:, 0:sz]), R=["sd2"], W=["sd2"])]

                    def piece(j2):
                        S.op("dve", lambda e: e.scalar_tensor_tensor(out=yT[:, j2, 0:sz], in0=yT[:, j2, 0:sz], scalar=cv[:, gcol + j2:gcol + j2 + 1],
                                                                      in1=sd2[:, 0:sz], op0=ALU.mult, op1=ALU.mult), R=[f"yT:{j2}", "sd2", "cv"], W=[f"yT:{j2}"])
                        S.op("pool", lambda e: e.tensor_tensor(out=xb_t[:, j2, 0:sz], in0=yT[:, j2, 0:sz], in1=xb_t[:, j2, 0:sz], op=ALU.add),
                             R=[f"yT:{j2}", xkey], W=[f"{xkey}:{j2}"])
                    for j2 in range(8):
                        pieces.append(lambda j2=j2: piece(j2))
                    pieces.append(lambda: S.dma("pool", "stx", dst_ap, xb_t[:, :, 0:sz], R=[xkey] + [f"{xkey}:{j2}" for j2 in range(8)], W=["xdst"]))
                    return pieces

                def post_block(st_bufs, emit_y, sz, gcol, xb_t, xkey, dst_ap):
                    post_head(st_bufs, emit_y, sz)
                    for pc_ in post_tail(st_bufs, sz, gcol, xb_t, xkey, dst_ap):
                        pc_()

                blks = blocks_of(TFL[l])
                nblk = len(blks)

                if not on(l, "D"):
                    break
                with ExitStack() as st:
                    wg = sb(st, "wg", [128, 8, 3072], BF16)
                    wbr = sb(st, "wbr", [128, 6, D], BF16)
                    wo = sb(st, "wo", [128, 8, D], BF16)
                    hb = [sb(st, f"hb{i}", [128, 8, 512], BF16) for i in range(2)]
                    ob_ = [sb(st, f"ob{i}", [128, 6, 512], BF16) for i in range(2)]
                    xbufD = sb(st, "xbufD", [128, 8, 512], F32)
                    sig = [sb(st, f"sig{i}", [128, 512], F32) for i in range(2)]
                    accs = [sb(st, f"acc{i}", [128, 512], F32) for i in range(2)]
                    tq = [sb(st, f"tq{i}", [128, 512], F32) for i in range(6)]
                    mT = sb(st, "mT", [128, 8, 512], BF16)
                    yT = sb(st, "yT", [128, 8, 512], F32)
                    ysq = sb(st, "ysq", [128, 8, 512], BF16)
                    sd2 = sb(st, "sd2", [128, 512], F32)
                    pG = [ps(st, f"pG{i}", [128, 512]) for i in range(2)]
                    pP = [ps(st, f"pP{i}", [128, 512]) for i in range(2)]
                    pY = [ps(st, f"pY{i}", [128, 512]) for i in range(2)]
                    pSS = ps(st, "pSS", [128, 512])
                    win = w_in_d[l].rearrange("(c p) n -> p c n", p=128)
                    for hq in range(2):
                        for br in range(3):
                            c0_ = br * 1024 + hq * 512
                            S.dma("pool", f"wg{hq}", wg[:, :, c0_:c0_ + 512], win[:, :, 2560 + c0_:2560 + c0_ + 512], W=[f"wg{hq}"])
                        if hq == 0:
                            S.dma("pool", "wbr", wbr[:, 0:3, :], w_bra_d[l].rearrange("(c p) n -> p c n", p=128), W=["wbr"])
                            S.dma("pool", "wbr", wbr[:, 3:4, :], w_brb_d[l].rearrange("(c p) n -> p c n", p=128), W=["wbr"])
                            S.dma("pool", "wbr", wbr[:, 4:6, :], w_brm_d[l].rearrange("(c p) n -> p c n", p=128), W=["wbr"])
                    S.dma("pool", "wo", wo[:], w_out_d[l].rearrange("(c p) n -> p c n", p=128), W=["wo"])
                    brk = [(0, 3), (3, 4), (4, 6)]

                    def load_blk(n):
                        s = n % 2
                        t0, sz = blks[n]
                        S.dma("sp", f"ldh{s}", hb[s][:, :, 0:sz], xview(hT_d, t0, sz), W=[f"hb{s}"])
                        S.dma("sp", f"ldh{s}", ob_[s][:, :, 0:sz], o_d[:, :, t0:t0 + sz], W=[f"ob{s}"])
                    load_blk(0)
                    gc = 0
                    pend = []
                    S.dma("sp", "ldxD", xbufD[:, :, 0:blks[0][1]], xview(xsrc, *blks[0]), W=["xbufD"])
                    for n in range(nblk):
                        s = n % 2
                        t0, sz = blks[n]
                        if n + 1 < nblk:
                            load_blk(n + 1)
                        for j in range(8):
                            for br in range(3):
                                if pend and (j * 3 + br) % 2 == 1:
                                    pend.pop(0)()
                                g_ = pG[gc % 2]
                                p_ = pP[gc % 2]
                                sg_ = sig[gc % 2]
                                for c in range(8):
                                    S.op("pe", lambda e, c=c, g_=g_, br=br, j=j: e.matmul(g_[:, 0:sz], lhsT=wg[:, c, br * 1024 + j * 128:br * 1024 + (j + 1) * 128],
                                                                                          rhs=hb[s][:, c, 0:sz], start=(c == 0), stop=(c == 7)),
                                         R=[f"wg{j // 4}", f"hb{s}"], W=[f"pG{gc % 2}"], inc=(c == 7))
                                k0, k1 = brk[br]
                                for k in range(k0, k1):
                                    S.op("pe", lambda e, k=k, p_=p_, j=j: e.matmul(p_[:, 0:sz], lhsT=wbr[:, k, j * 128:(j + 1) * 128], rhs=ob_[s][:, k, 0:sz],
                                                                                    start=(k == k0), stop=(k == k1 - 1)),
                                         R=["wbr", f"ob{s}"], W=[f"pP{gc % 2}"], inc=(k == k1 - 1))
                                bcol = cb + 32 + br * 8 + j
                                S.op("act", lambda e, g_=g_, sg_=sg_, bcol=bcol: e.activation(out=sg_[:, 0:sz], in_=g_[:, 0:sz], func=AF.Sigmoid, bias=cv[:, bcol:bcol + 1]),
                                     R=[f"pG{gc % 2}", "cv"], W=[f"sig{gc % 2}"])
                                tq_ = tq[(j % 2) * 3 + br]
                                tqk = f"tq{(j % 2) * 3 + br}"
                                S.op("dve", lambda e, sg_=sg_, p_=p_, tq_=tq_: e.tensor_tensor(out=tq_[:, 0:sz], in0=p_[:, 0:sz], in1=sg_[:, 0:sz], op=ALU.mult),
                                     R=[f"pP{gc % 2}", f"sig{gc % 2}"], W=[tqk])
                                if br == 2:
                                    jb = (j % 2) * 3
                                    ac_ = accs[j % 2]
                                    S.op("pool", lambda e, jb=jb, ac_=ac_: e.tensor_tensor(out=ac_[:, 0:sz], in0=tq[jb][:, 0:sz], in1=tq[jb + 1][:, 0:sz], op=ALU.add),
                                         R=[f"tq{jb}", f"tq{jb + 1}"], W=[f"acc{j % 2}"])
                                    S.op("pool", lambda e, jb=jb, ac_=ac_, j=j: e.tensor_tensor(out=mT[:, j, 0:sz], in0=ac_[:, 0:sz], in1=tq[jb + 2][:, 0:sz], op=ALU.add),
                                         R=[f"acc{j % 2}", f"tq{jb + 2}"], W=["mT"])
                                gc += 1

                        def emit_y(j2, p, pkey):
                            for j in range(8):
                                S.op("pe", lambda e, j=j: e.matmul(p[:, 0:sz], lhsT=wo[:, j, j2 * 128:(j2 + 1) * 128], rhs=mT[:, j, 0:sz], start=(j == 0), stop=(j == 7)),
                                     R=["wo", "mT"], W=[pkey], inc=(j == 7))
                        while pend:
                            pend.pop(0)()
                        post_head((yT, ysq, sd2, pY, pSS), emit_y, sz)
                        pend = post_tail((yT, ysq, sd2, pY, pSS), sz, cb + 8, xbufD, "xbufD", xview(xmid, t0, sz))
                        if n + 1 < nblk:
                            t1_, sz1_ = blks[n + 1]
                            pend.append(lambda t1_=t1_, sz1_=sz1_: S.dma("sp", "ldxD", xbufD[:, :, 0:sz1_], xview(xsrc, t1_, sz1_), W=["xbufD"]))
                    while pend:
                        pend.pop(0)()
                    S.barrier()

                if not on(l, "E"):
                    break
                TE = TFL[l]
                fblks = blks if l == 0 else blks[:OWN // 512]
                with ExitStack() as st:
                    hT = sb(st, "h2T", [128, 8, TE], BF16)
                    with ExitStack() as st2:
                        xbuf = [sb(st2, f"xbufE{i}", [128, 8, 512], F32) for i in range(3)]
                        nbufs = dict(sq=[sb(st2, f"sqE{i}", [128, 8, 512], BF16) for i in range(2)],
                                     sd=[sb(st2, f"sdE{i}", [128, 512], F32) for i in range(2)],
                                     tp=[sb(st2, f"tpE{i}", [128, 3, 512], F32) for i in range(2)],
                                     pss=[ps(st2, f"pssE{i}", [128, 512]) for i in range(2)])
                        S.dma("sp", "ldx0", xbuf[0][:, :, 0:blks[0][1]], xview(xmid, *blks[0]), W=["xbufE0"])
                        for n in range(nblk + 1):
                            if n + 1 < nblk:
                                S.dma("sp", f"ldx{(n + 1) % 3}", xbuf[(n + 1) % 3][:, :, 0:blks[n + 1][1]], xview(xmid, *blks[n + 1]), W=[f"xbufE{(n + 1) % 3}"])
                            if n < nblk:
                                t0, sz = blks[n]
                                norm_p1(xbuf[n % 3][:, :, 0:sz], f"xbufE{n % 3}", sz, nbufs, n)
                            if n >= 1:
                                m_ = n - 1
                                t0, sz = blks[m_]
                                norm_p2(xbuf[m_ % 3][:, :, 0:sz], f"xbufE{m_ % 3}", sz, cb + 16, hT[:, :, t0:t0 + sz], f"h2T:{m_}", nbufs, m_)
                            if n < nblk:
                                norm_p1b(blks[n][1], nbufs, n)
                        S.barrier()
                    pA = [ps(st, f"pA{i}", [128, 512]) for i in range(2)]
                    pB = [ps(st, f"pB{i}", [128, 512]) for i in range(2)]
                    wa = [sb(st, f"wa{i}", [128, 8, 256], BF16) for i in range(2)]
                    wb_ = [sb(st, f"wb_{i}", [128, 8, 256], BF16) for i in range(2)]
                    Ua = [sb(st, f"Ua{i}", [128, TE + 2], F32) for i in range(2)]
                    Ub = [sb(st, f"Ub{i}", [128, TE + 2], F32) for i in range(2)]
                    nfb = len(fblks)
                    ca = [sb(st, f"ca{i}", [128, 512], F32) for i in range(nfb)]
                    cbt = [sb(st, f"cbt{i}", [128, 512], F32) for i in range(nfb)]
                    fst = [sb(st, f"fst{i}", [128, TE], BF16) for i in range(2)]
                    wup = w_up_d[l].rearrange("(c p) n -> p c n", p=128)
                    TFF = fblks[-1][0] + fblks[-1][1]

                    def load_w(gi):
                        s = gi % 2
                        S.dma("pool", f"wa{s}", wa[s][:], wup[:, :, gi * 256:(gi + 1) * 256], W=[f"wa{s}"])
                        S.dma("pool", f"wa{s}", wb_[s][:], wup[:, :, DFF + gi * 256:DFF + (gi + 1) * 256], W=[f"wb_{s}"])
                    load_w(0)
                    load_w(1)
                    for i in range(2):
                        for U, nm in ((Ua, "Ua"), (Ub, "Ub")):
                            S.op("pool", lambda e, U=U, i=i: e.memset(U[i][:, 0:1], 0.0), W=[f"{nm}{i}"])
                            S.op("pool", lambda e, U=U, i=i: e.memset(U[i][:, TE + 1:TE + 2], 0.0), W=[f"{nm}{i}"])
                    cw = cb + 56
                    cbias = cb + 188
                    units = [(jp, n) for jp in range(22) for n in range(nblk)]
                    pcs = [0]

                    def stage0(jp, n):
                        gi, jj = jp // 2, jp % 2
                        s, u = gi % 2, jp % 2
                        if n == 0 and jj == 0 and gi >= 1 and gi + 1 < 11:
                            load_w(gi + 1)
                        t0, sz = blks[n]
                        pc = pcs[0]
                        pcs[0] += 1
                        a_, b_ = pA[pc % 2], pB[pc % 2]
                        for (p_, w_, wkey, pkey) in ((a_, wa[s], f"wa{s}", f"pA{pc % 2}"), (b_, wb_[s], f"wb_{s}", f"pB{pc % 2}")):
                            for c in range(8):
                                S.op("pe", lambda e, c=c, p_=p_, w_=w_: e.matmul(p_[:, 0:sz], lhsT=w_[:, c, jj * 128:(jj + 1) * 128], rhs=hT[:, c, t0:t0 + sz],
                                                                                 start=(c == 0), stop=(c == 7)),
                                     R=[wkey, f"h2T:{n}"], W=[pkey], inc=(c == 7))
                        S.op("act", lambda e: e.activation(out=Ua[u][:, 1 + t0:1 + t0 + sz], in_=a_[:, 0:sz], func=AF.Copy),
                             R=[f"pA{pc % 2}"], W=[f"Ua{u}:{n}"])
                        S.op("act", lambda e: e.activation(out=Ub[u][:, 1 + t0:1 + t0 + sz], in_=b_[:, 0:sz], func=AF.Copy),
                             R=[f"pB{pc % 2}"], W=[f"Ub{u}:{n}"])

                    def ukeys(nm, u, n):
                        return [f"{nm}{u}:{k}" for k in (n - 1, n, n + 1) if 0 <= k < nblk] + [f"{nm}{u}"]

                    def stage1(jp, n):
                        if n >= nfb:
                            return
                        u = jp % 2
                        base, sz = fblks[n]
                        for (U, nm, dstt, dkey, ch) in ((Ua, "Ua", ca, "ca", jp), (Ub, "Ub", cbt, "cbt", 22 + jp)):
                            w0 = cv[:, cw + ch:cw + ch + 1]
                            w1 = cv[:, cw + 44 + ch:cw + 44 + ch + 1]
                            w2 = cv[:, cw + 88 + ch:cw + 88 + ch + 1]
                            bb = cv[:, cbias + ch:cbias + ch + 1]
                            o = dstt[n]
                            S.op("act", lambda e, o=o, U=U, w1=w1, bb=bb: e.activation(out=o[:, 0:sz], in_=U[u][:, base + 1:base + 1 + sz], func=AF.Identity,
                                                                                        scale=w1, bias=bb),
                                 R=ukeys(nm, u, n) + ["cv"], W=[f"{dkey}{n}"])
                            S.op("dve", lambda e, o=o, U=U, w0=w0: e.scalar_tensor_tensor(out=o[:, 0:sz], in0=U[u][:, base:base + sz], scalar=w0, in1=o[:, 0:sz],
                                                                                          op0=ALU.mult, op1=ALU.add),
                                 R=ukeys(nm, u, n) + ["cv", f"{dkey}{n}"], W=[f"{dkey}{n}"])
                            S.op("dve", lambda e, o=o, U=U, w2=w2: e.scalar_tensor_tensor(out=o[:, 0:sz], in0=U[u][:, base + 2:base + 2 + sz], scalar=w2, in1=o[:, 0:sz],
                                                                                          op0=ALU.mult, op1=ALU.add),
                                 R=ukeys(nm, u, n) + ["cv", f"{dkey}{n}"], W=[f"{dkey}{n}"])

                    def stage2(jp, n):
                        if n >= nfb:
                            return
                        sz = fblks[n][1]
                        S.op("act", lambda e: e.activation(out=ca[n][:, 0:sz], in_=ca[n][:, 0:sz], func=AF.Gelu_apprx_tanh), R=[f"ca{n}"], W=[f"ca{n}"])

                    def stage3(jp, n):
                        if n >= nfb:
                            return
                        u = jp % 2
                        base, sz = fblks[n]
                        S.op("pool", lambda e: e.tensor_tensor(out=fst[u][:, base:base + sz], in0=ca[n][:, 0:sz], in1=cbt[n][:, 0:sz], op=ALU.mult),
                             R=[f"ca{n}", f"cbt{n}"], W=[f"fst{u}"])
                        if n == nfb - 1:
                            S.dma("pool", "stf", f_d[:, jp, 0:TFF], fst[u][:, 0:TFF], R=[f"fst{u}"], W=[f"fd:{jp}"])

                    lags = (0, nblk, nblk + 2, nblk + 3)
                    stages = (stage0, stage1, stage2, stage3)
                    for g in range(len(units) + lags[-1]):
                        for lag, fn in zip(lags, stages):
                            k = g - lag
                            if 0 <= k < len(units):
                                fn(*units[k])
                    S.barrier()

                if not on(l, "F"):
                    break
                with ExitStack() as st:
                    wd = sb(st, "wd", [128, 22, D], BF16)
                    fb = [sb(st, f"fb{i}", [128, 22, 512], BF16) for i in range(2)]
                    xbuf = [sb(st, f"xbufF{i}", [128, 8, 512], F32) for i in range(2)]
                    yT = sb(st, "yTF", [128, 8, 512], F32)
                    ysq = sb(st, "ysqF", [128, 8, 512], BF16)
                    sd2 = sb(st, "sd2F", [128, 512], F32)
                    pY = [ps(st, f"pYF{i}", [128, 512]) for i in range(2)]
                    pSS = ps(st, "pSSF", [128, 512])
                    wdn = w_dn_d[l].rearrange("(c p) n -> p c n", p=128)
                    for h in range(4):
                        S.dma("pool", f"wd{h}", wd[:, :, h * 256:(h + 1) * 256], wdn[:, :, h * 256:(h + 1) * 256], W=[f"wd{h}"])

                    def load_blkF(n):
                        s = n % 2
                        t0, sz = fblks[n]
                        S.dma("sp", f"ldf{s}", fb[s][:, :, 0:sz], f_d[:, :, t0:t0 + sz], W=[f"fb{s}"])
                        S.dma("sp", f"ldf{s}", xbuf[s][:, :, 0:sz], xview(xmid, t0, sz), W=[f"xbufF{s}"])
                    load_blkF(0)
                    for n in range(len(fblks)):
                        s = n % 2
                        t0, sz = fblks[n]
                        if n + 1 < len(fblks):
                            load_blkF(n + 1)

                        def emit_y(j2, p, pkey):
                            for j in range(22):
                                S.op("pe", lambda e, j=j: e.matmul(p[:, 0:sz], lhsT=wd[:, j, j2 * 128:(j2 + 1) * 128], rhs=fb[s][:, j, 0:sz], start=(j == 0), stop=(j == 21)),
                                     R=[f"wd{j2 // 2}", f"fb{s}"], W=[pkey], inc=(j == 21))
                        post_block((yT, ysq, sd2, pY, pSS), emit_y, sz, cb + 24, xbuf[s], f"xbufF{s}", xview(xdst, t0, sz))
                    S.barrier()
        except _Stop:
            pass
        S.barrier()
    return nc


def _colvec(v):
    return np.ascontiguousarray(np.asarray(v, np.float32).reshape(-1, 128).T)


def _na_tables(rpb, rev):
    out = np.full((5, 6, 128, 640), NEG, np.float32)
    for v, (m, kt0) in enumerate([(0, 0), (1, 0), (2, 0)]):
        qtok = m * 128 + np.arange(128)
        ktok = kt0 * 128 + np.arange(640)
        if rev:
            qtok = SEQ - 1 - qtok
            ktok = SEQ - 1 - ktok
        i, c = qtok // 64, qtok % 64
        kr, kc = ktok // 64, ktok % 64
        r0 = np.clip(i - 4, 0, 56)
        c0 = np.clip(c - 8, 0, 48)
        valid = ((kr[:, None] >= r0[None]) & (kr[:, None] < r0[None] + 8) &
                 (kc[:, None] >= c0[None]) & (kc[:, None] < c0[None] + 16))
        ri = np.clip(kr[:, None] - i[None] + 7, 0, 14)
        ci = np.clip(kc[:, None] - c[None] + 15, 0, 30)
        tab = np.where(valid[None], rpb[:, ri, ci], np.float32(NEG)).astype(np.float32)
        out[v] = tab.reshape(6, 5, 128, 128).transpose(0, 2, 1, 3).reshape(6, 128, 640)
    return out


def _dil_tables():
    slopes = 2.0 ** (-8.0 * np.arange(1, 7, dtype=np.float32) / 6)
    kp = np.arange(128)[:, None, None]
    j = np.arange(3)[None, :, None]
    qp = np.arange(128)[None, None, :]
    delta = (j - 1) * 128 + kp - qp
    tabs = np.empty((128, 6, 384), np.float32)
    for h in range(6):
        d = DIL[h // 2]
        t = np.where(np.abs(delta) <= 64, -slopes[h] * (d * np.abs(delta)).astype(np.float32), np.float32(NEG))
        tabs[:, h, :] = t.reshape(128, 384)
    return tabs


_NC_CACHE = {}


def kernel(x, mem, mem_norm_g, g_pre_mix, w_in, rpb_na, w_mem_kv, b_gate, w_br_a, w_br_b, w_br_m, w_out,
           g_post_mix, g_pre_ffn, w_up, conv_w, conv_b, w_down, g_post_ffn):
    f = lambda a: np.ascontiguousarray(np.asarray(a, np.float32))
    x = f(x)
    mem = f(mem)
    B = x.shape[0]
    assert B * 2 == N_CORES
    per_hf = []
    for hf in range(2):
        cols = [_colvec(mem_norm_g)]
        for l in range(2):
            cols += [_colvec(g_pre_mix[l]), _colvec(g_post_mix[l]), _colvec(g_pre_ffn[l]), _colvec(g_post_ffn[l])]
            cols += [_colvec(np.asarray(b_gate[l]).reshape(-1))]
            cw = np.asarray(conv_w[l], np.float32)
            if hf:
                cw = cw[::-1]
            cols += [_colvec(np.ascontiguousarray(cw).reshape(-1))]
            cols += [_colvec(conv_b[l])]
        cv = np.ascontiguousarray(np.concatenate(cols, axis=1))
        assert cv.shape == (128, NCV), cv.shape
        nab = np.stack([_na_tables(np.asarray(rpb_na[l], np.float32), bool(hf)) for l in range(2)])
        per_hf.append((cv, nab))
    shared = {"dilb": _dil_tables(), "ident": np.eye(128, dtype=np.float32), "w_in": f(w_in), "w_mem_kv": f(w_mem_kv), "w_br_a": f(w_br_a),
              "w_br_b": f(w_br_b), "w_br_m": f(w_br_m), "w_out": f(w_out), "w_up": f(w_up), "w_down": f(w_down)}
    in_maps = []
    for c in range(N_CORES):
        b, hf = c // 2, c % 2
        m = dict(shared)
        m["cv"], m["nab"] = per_hf[hf]
        xs = x[b][::-1] if hf else x[b]
        m["xT"] = np.ascontiguousarray(xs.reshape(NB, 512, 8, 128).transpose(0, 3, 2, 1))
        m["memT"] = np.ascontiguousarray(mem[b].T)
        in_maps.append(m)
    if "nc" not in _NC_CACHE:
        _NC_CACHE["nc"] = build_nc()
    res = run_bass_kernel_spmd(_NC_CACHE["nc"], in_maps, core_ids=list(range(N_CORES)))
    out = np.empty((B, SEQ, D), np.float32)
    for c in range(N_CORES):
        b, hf = c // 2, c % 2
        o = res.results[c]["outT"].transpose(0, 3, 2, 1).reshape(OWN, D)
        if hf:
            out[b, SEQ - OWN:] = o[::-1]
        else:
            out[b, :OWN] = o
    return out
```

```python
import numpy as np
import ml_dtypes
from contextlib import ExitStack
import concourse.bass as bass
import concourse.mybir as mybir
from concourse.bass_utils import run_bass_kernel_spmd

F32 = mybir.dt.float32
BF16 = mybir.dt.bfloat16
AF = mybir.ActivationFunctionType
ALU = mybir.AluOpType

D = 1024
SEQ = 4096
NMEM = 256
DIN = 5632
DFF = 2816
NEG = -30000.0
N_CORES = 8
OWN = 2048
TFL = (3200, 2176)
T = SEQ
NB = T // 512
NT = T // 128
NCV = 8 + 2 * 232
DIL = (1, 4, 16)


class _Stop(Exception):
    pass


class Sched:
    def __init__(self, nc, stack):
        self.nc = nc
        self.stack = stack
        self.eng = {"pe": nc.tensor, "act": nc.scalar, "dve": nc.vector, "pool": nc.gpsimd, "sp": nc.sync}
        self.sem = {}
        self.val = {}
        self.seen = {e: {} for e in self.eng}
        self.lw = {}
        self.rd = {}
        self.pending = {e: False for e in self.eng}
        self.halt = False
        for e in self.eng:
            self.sem[e] = stack.enter_context(nc.semaphore("sem_" + e))
            self.val[e] = 0

    def _stream(self, name):
        if name not in self.sem:
            self.sem[name] = self.stack.enter_context(self.nc.semaphore("dq_" + name))
            self.val[name] = 0
        return name

    def _deps(self, eng, R, W):
        deps = {}

        def add(p):
            k, v = p
            if k not in self.eng:
                v = self.val[k]
            if deps.get(k, 0) < v:
                deps[k] = v
        for k in R:
            if k in self.lw:
                add(self.lw[k])
        for k in W:
            if k in self.lw:
                add(self.lw[k])
            for p in self.rd.get(k, {}).items():
                add(p)
        h = self.eng[eng]
        for k, v in deps.items():
            if k == eng and eng == "pe":
                continue
            if self.seen[eng].get(k, 0) < v:
                h.wait_ge(self.sem[k], v)
                self.seen[eng][k] = v

    def _mark(self, me, R, W):
        for k in R:
            d = self.rd.setdefault(k, {})
            if d.get(me[0], 0) < me[1]:
                d[me[0]] = me[1]
        for k in W:
            self.lw[k] = me
            self.rd[k] = {}

    def op(self, eng, fn, R=(), W=(), inc=True):
        if self.halt:
            return
        self._deps(eng, R, W)
        ins = fn(self.eng[eng])
        if inc:
            ins.then_inc(self.sem[eng], 1)
            self.val[eng] += 1
            me = (eng, self.val[eng])
            self.pending[eng] = False
        else:
            me = (eng, self.val[eng] + 1)
            self.pending[eng] = True
        self._mark(me, R, W)

    def dma(self, q, stream, out, in_, R=(), W=()):
        if self.halt:
            return
        s = self._stream(stream)
        self._deps(q, R, W)
        self.eng[q].dma_start(out=out, in_=in_).then_inc(self.sem[s], 16)
        self.val[s] += 16
        self._mark((s, self.val[s]), R, W)

    def barrier(self):
        if self.halt:
            return
        for e in self.eng:
            assert not self.pending[e], e
        for e, h in self.eng.items():
            for k, v in self.val.items():
                if k == e:
                    continue
                if v > 0 and self.seen[e].get(k, 0) < v:
                    h.wait_ge(self.sem[k], v)
                    self.seen[e][k] = v
        self.lw = {}
        self.rd = {}


def build_nc(dbg=None, stop=None):
    nc = bass.Bass("TRN2", target_bir_lowering=False)

    def din(name, shape, dt=F32):
        return nc.dram_tensor(name, list(shape), dt, kind="ExternalInput").ap()

    def dscr(name, shape, dt):
        kind = "ExternalOutput" if (dbg and name in dbg) else "Internal"
        return nc.dram_tensor(name, list(shape), dt, kind=kind).ap()

    xT_d = din("xT", [NB, 128, 8, 512])
    memT_d = din("memT", [D, NMEM])
    cv_d = din("cv", [128, NCV])
    dilb_d = din("dilb", [128, 6, 384])
    ident_d = din("ident", [128, 128])
    nab_d = din("nab", [2, 5, 6, 128, 640])
    w_in_d = din("w_in", [2, D, DIN])
    w_kv_d = din("w_mem_kv", [2, D, 512])
    w_bra_d = din("w_br_a", [2, 384, D])
    w_brb_d = din("w_br_b", [2, 128, D])
    w_brm_d = din("w_br_m", [2, 256, D])
    w_out_d = din("w_out", [2, D, D])
    w_up_d = din("w_up", [2, D, DIN])
    w_dn_d = din("w_down", [2, DFF, D])
    out_d = nc.dram_tensor("outT", [OWN // 512, 128, 8, 512], F32, kind="ExternalOutput").ap()

    xa_d = dscr("xa", [NB, 128, 8, 512], F32)
    xb_d = dscr("xb", [NB, 128, 8, 512], F32)
    hT_d = dscr("hTd", [NB, 128, 8, 512], BF16)
    qk_d = dscr("qk", [14, 128, T], BF16)
    vp_d = dscr("vpad", [6, T, 256], BF16)
    o_d = dscr("oT", [128, 6, T], BF16)
    f_d = dscr("fT", [128, 22, T], BF16)

    def xview(ap, t0, sz=None):
        if sz is None:
            t0, sz = t0 * 512, 512
        assert t0 // 512 == (t0 + sz - 1) // 512
        return ap[t0 // 512, :, :, t0 % 512:t0 % 512 + sz]

    def blocks_of(tf):
        b = [(n * 512, 512) for n in range(tf // 512)]
        if tf % 512:
            b.append((tf - tf % 512, tf % 512))
        return b

    with ExitStack() as top:
        S = Sched(nc, top)

        uid = [0]

        def sb(stack, name, shape, dt):
            uid[0] += 1
            return stack.enter_context(nc.sbuf_tensor(f"s{uid[0]}_{name}", list(shape), dt))

        def ps(stack, name, shape, dt=F32):
            uid[0] += 1
            return stack.enter_context(nc.psum_tensor(f"p{uid[0]}_{name}", list(shape), dt))

        ones = sb(top, "ones", [128, 128], BF16)
        onesh = sb(top, "onesh", [128, 2, 128], BF16)
        eps = sb(top, "eps", [128, 1], F32)
        cv = sb(top, "cv", [128, NCV], F32)
        dilbb = sb(top, "dilbb", [128, 6, 384], BF16)
        ident = sb(top, "ident", [128, 128], BF16)
        mnT = sb(top, "mnT", [128, 8, NMEM], BF16)
        kmT_l = [sb(top, f"kmT{l}", [128, 2, NMEM], BF16) for l in range(2)]
        vmp_l = [sb(top, f"vmp{l}", [128, 2, 2, 256], BF16) for l in range(2)]
        S.op("dve", lambda e: e.memset(ones[:], 1.0), W=["ones"])
        S.op("dve", lambda e: e.memset(onesh[:], 0.0), W=["onesh"])
        S.op("dve", lambda e: e.memset(onesh[:, 0, 0:64], 1.0), W=["onesh"])
        S.op("dve", lambda e: e.memset(onesh[:, 1, 64:128], 1.0), W=["onesh"])
        S.op("dve", lambda e: e.memset(eps[:], 1e-6), W=["eps"])
        S.dma("sp", "const", cv[:], cv_d[:, :], W=["cv"])
        S.dma("pool", "constc", dilbb[:], dilb_d[:, :, :], W=["dilbb"])
        S.dma("pool", "constc", ident[:], ident_d[:, :], W=["ident"])
        n0_, o0_ = TFL[0] // 512, TFL[0] % 512
        if o0_:
            S.dma("sp", "const", xb_d[n0_, :, :, o0_:512], xT_d[n0_, :, :, o0_:512], W=["xbseed"])
            n0_ += 1
        for n_ in range(n0_, NB):
            S.dma("sp", "const", xb_d[n_], xT_d[n_], W=["xbseed"])
        S.barrier()

        def norm_p1(xb_ap, xkey, N, nb, i):
            k = i % len(nb["sq"])
            sq, pss, sd = nb["sq"][k], nb["pss"][k], nb["sd"][k]
            sqkey, psskey, sdkey = f"nsq{k}", f"npss{k}", f"nsd{k}"
            S.op("act", lambda e: e.activation(out=sq[:, :, 0:N], in_=xb_ap, func=AF.Square), R=[xkey], W=[sqkey])
            for c in range(8):
                S.op("pe", lambda e, c=c: e.matmul(pss[:, 0:N], lhsT=ones[:], rhs=sq[:, c, 0:N], start=(c == 0), stop=(c == 7)),
                     R=[sqkey, "ones"], W=[psskey], inc=(c == 7))
            S.op("act", lambda e: e.activation(out=sd[:, 0:N], in_=pss[:, 0:N], func=AF.Sqrt, scale=1.0 / D, bias=eps[:]),
                 R=[psskey, "eps"], W=[sdkey])

        def norm_p1b(N, nb, i):
            k = i % len(nb["sq"])
            sd = nb["sd"][k]
            S.op("dve", lambda e: e.reciprocal(out=sd[:, 0:N], in_=sd[:, 0:N]), R=[f"nsd{k}"], W=[f"nsd{k}"])

        def norm_p2(xb_ap, xkey, N, gcol, h_ap, hkey, nb, i):
            k = i % len(nb["sq"])
            sd = nb["sd"][k]
            tp = nb["tp"][k] if nb.get("tp") else None
            sdkey, tpkey = f"nsd{k}", f"ntp{k}"
            npool = 2 if tp is not None else 0
            for c in range(8 - npool, 8):
                cc = c - (8 - npool)
                S.op("pool", lambda e, c=c, cc=cc: e.tensor_tensor(out=tp[:, cc, 0:N], in0=xb_ap[:, c, :], in1=sd[:, 0:N], op=ALU.mult),
                     R=[xkey, sdkey], W=[tpkey])
            for c in range(8 - npool):
                S.op("dve", lambda e, c=c: e.scalar_tensor_tensor(out=h_ap[:, c, :], in0=xb_ap[:, c, :], scalar=cv[:, gcol + c:gcol + c + 1],
                                                                   in1=sd[:, 0:N], op0=ALU.mult, op1=ALU.mult),
                     R=[xkey, sdkey, "cv"], W=[hkey])
            for c in range(8 - npool, 8):
                cc = c - (8 - npool)
                S.op("act", lambda e, c=c, cc=cc: e.activation(out=h_ap[:, c, :], in_=tp[:, cc, 0:N], func=AF.Copy, scale=cv[:, gcol + c:gcol + c + 1]),
                     R=[tpkey, "cv"], W=[hkey])

        def norm_block(xb_ap, xkey, N, gcol, h_ap, hkey, nb, i):
            norm_p1(xb_ap, xkey, N, nb, i)
            norm_p1b(N, nb, i)
            norm_p2(xb_ap, xkey, N, gcol, h_ap, hkey, nb, i)

        with ExitStack() as st:
            mx = sb(st, "mx", [128, 8, NMEM], F32)
            msq = sb(st, "msq", [128, 8, NMEM], BF16)
            msd = sb(st, "msd", [128, NMEM], F32)
            mps = ps(st, "mps", [128, 512])
            S.dma("sp", "ld0", mx[:], memT_d.rearrange("(c p) t -> p c t", p=128), W=["mx"])
            norm_block(mx[:], "mx", NMEM, 0, mnT[:], "mnT", dict(sq=[msq], pss=[mps], sd=[msd]), 0)
            S.barrier()

        def on(l, ph):
            return stop is None or (l, ph) <= stop

        def chk(l, tag):
            if stop == (l, tag):
                S.barrier()
                S.halt = True

        try:
            for l in range(2):
                cb = 8 + l * 232
                if not on(l, "A"):
                    break
                xsrc = xT_d if l == 0 else xb_d
                xmid = xa_d
                xdst = xb_d if l == 0 else out_d

                with ExitStack() as st:
                    hT = sb(st, "hT", [128, 8, T], BF16)
                    pp = [ps(st, f"pp{i}", [128, 512]) for i in range(2)]
                    pv = [ps(st, f"pv{i}", [128, 512]) for i in range(2)]
                    wbuf = [sb(st, f"wbuf{i}", [128, 8, 512], BF16) for i in range(2)]
                    wv = sb(st, "wv", [128, 8, 768], BF16)
                    wkv = sb(st, "wkv", [128, 8, 512], BF16)
                    kmT, vmp = kmT_l[l], vmp_l[l]
                    NVST = 8

                    win = w_in_d[l].rearrange("(c p) n -> p c n", p=128)
                    S.dma("pool", "wv", wv[:, :, 0:384], win[:, :, 768:1152], W=["wv"])
                    S.dma("pool", "wv", wv[:, :, 384:768], win[:, :, 1920:2304], W=["wv"])
                    S.dma("pool", "wkv", wkv[:], w_kv_d[l].rearrange("(c p) n -> p c n", p=128), W=["wkv"])
                    groups = [(0, 512, [0, 1, 2, 3]), (512, 256, [4, 5]), (1152, 512, [6, 7, 8, 9]),
                              (1664, 256, [10, 11]), (2304, 256, [12, 13])]

                    def load_group(gi):
                        c0, w, _ = groups[gi]
                        S.dma("pool", f"wb{gi % 2}", wbuf[gi % 2][:, :, 0:w], win[:, :, c0:c0 + w], W=[f"wbuf{gi % 2}"])
                    load_group(0)
                    load_group(1)

                    with ExitStack() as st2:
                        xbuf = [sb(st2, f"xbuf{i}", [128, 8, 512], F32) for i in range(3)]
                        nbufs = dict(sq=[sb(st2, f"sq{i}", [128, 8, 512], BF16) for i in range(2)],
                                     sd=[sb(st2, f"sd{i}", [128, 512], F32) for i in range(2)],
                                     tp=[sb(st2, f"tp{i}", [128, 3, 512], F32) for i in range(2)],
                                     pss=[ps(st2, f"pss{i}", [128, 512]) for i in range(2)])
                        S.dma("sp", "ldx0", xbuf[0][:], xview(xsrc, 0), W=["xbuf0"])
                        for n in range(NB + 1):
                            if n + 1 < NB:
                                S.dma("sp", f"ldx{(n + 1) % 3}", xbuf[(n + 1) % 3][:], xview(xsrc, n + 1), W=[f"xbuf{(n + 1) % 3}"])
                            if n < NB:
                                norm_p1(xbuf[n % 3][:], f"xbuf{n % 3}", 512, nbufs, n)
                            if n >= 1:
                                m_ = n - 1
                                norm_p2(xbuf[m_ % 3][:], f"xbuf{m_ % 3}", 512, cb + 0, hT[:, :, m_ * 512:(m_ + 1) * 512], f"hT:{m_}", nbufs, m_)
                                S.dma("pool", "sthT", hT_d[m_], hT[:, :, m_ * 512:(m_ + 1) * 512], R=[f"hT:{m_}"], W=[f"hTd:{m_}"])
                            if n < NB:
                                norm_p1b(512, nbufs, n)
                        S.barrier()
                    stage = [sb(st, f"stage{i}", [128, T], BF16) for i in range(2)]
                    vst = [sb(st, f"vst{i}", [128, 3, 256], BF16) for i in range(NVST)]
                    for i in range(NVST):
                        S.op("pool", lambda e, i=i: e.memset(vst[i][:], 0.0), W=[f"vst{i}"])
                    hkeys = [f"hT:{n}" for n in range(NB)]
                    chk(l, "A1")

                    TFc = TFL[l]
                    nbq = -(-TFc // 512)
                    kmaxs = []
                    for g_ in range(3):
                        d_ = DIL[g_]
                        ntc_ = NT // d_
                        kmaxs.append(min(ntc_ - 1, min(ntc_, -(-TFc // (128 * d_)))))
                    nb_of = {}
                    for idx_ in (0, 1, 2, 6, 7, 8, 12, 13):
                        nb_of[idx_] = nbq
                    for idx_ in (3, 4, 5):
                        nb_of[idx_] = min(NB, -(-(TFc + 256) // 512))
                    for g_ in range(3):
                        nb_of[9 + g_] = min(NB, -(-(DIL[g_] * 128 * (kmaxs[g_] + 1)) // 512))
                    nva = min(NT, TFc // 128 + 2)
                    cnt = 0
                    ci = 0
                    for gi, (c0, w, idxs) in enumerate(groups):
                        wb = wbuf[gi % 2]
                        for jj, idx in enumerate(idxs):
                            sg = stage[ci % 2]
                            for n in range(nb_of[idx]):
                                p = pp[cnt % 2]
                                for c in range(8):
                                    S.op("pe", lambda e, c=c, p=p, jj=jj, n=n, wb=wb: e.matmul(p[:], lhsT=wb[:, c, jj * 128:(jj + 1) * 128],
                                                                                                  rhs=hT[:, c, n * 512:(n + 1) * 512], start=(c == 0), stop=(c == 7)),
                                         R=[f"wbuf{gi % 2}", f"hT:{n}"], W=[f"pp{cnt % 2}"], inc=(c == 7))
                                qs = 0.125 if idx in (0, 1, 2, 6, 7, 8, 12, 13) else 1.0
                                if cnt % 2 == 0:
                                    S.op("act", lambda e, p=p, sg=sg, n=n: e.activation(out=sg[:, n * 512:(n + 1) * 512], in_=p[:], func=AF.Copy, scale=qs),
                                         R=[f"pp{cnt % 2}"], W=[f"stage{ci % 2}"])
                                else:
                                    S.op("dve", lambda e, p=p, sg=sg, n=n: e.tensor_scalar(out=sg[:, n * 512:(n + 1) * 512], in0=p[:], scalar1=qs, scalar2=None,
                                                                                             op0=ALU.mult),
                                         R=[f"pp{cnt % 2}"], W=[f"stage{ci % 2}"])
                                cnt += 1
                            S.dma("pool", "stqk", qk_d[idx, :, 0:nb_of[idx] * 512], sg[:, 0:nb_of[idx] * 512], R=[f"stage{ci % 2}"], W=[f"qk:{idx}"])
                            ci += 1
                        if gi + 2 < len(groups):
                            load_group(gi + 2)

                    chk(l, "A2")
                    vc = 0
                    for t in range(nva):
                        p = pv[vc % 2]
                        v = vst[vc % NVST]
                        for c in range(8):
                            S.op("pe", lambda e, c=c, p=p, t=t: e.matmul(p[:, 0:384], lhsT=hT[:, c, t * 128:(t + 1) * 128], rhs=wv[:, c, 0:384],
                                                                          start=(c == 0), stop=(c == 7)),
                                 R=["wv", f"hT:{t // 4}"], W=[f"pv{vc % 2}"], inc=(c == 7))
                        dst = v[:].rearrange("p a (b d) -> p a b d", d=64)[:, :, 0:4:3, :]
                        src = p[:, 0:384].rearrange("p (a h d) -> p a h d", a=3, h=2)
                        S.op("dve" if vc % 2 else "act",
                             (lambda e, dst=dst, src=src: e.tensor_copy(out=dst, in_=src)) if vc % 2 else
                             (lambda e, dst=dst, src=src: e.activation(out=dst, in_=src, func=AF.Copy)),
                             R=[f"pv{vc % 2}"], W=[f"vst{vc % NVST}"])
                        S.dma("pool" if vc % 2 else "sp", f"stv{vc % NVST}", vp_d[0:3, t * 128:(t + 1) * 128, :].rearrange("a p x -> p a x"), v[:], R=[f"vst{vc % NVST}"], W=[f"vpd{vc}"])
                        vc += 1
                    for g in range(3):
                        d = DIL[g]
                        ntc = NT // d
                        for r in range(d):
                            for j in range(kmaxs[g] + 1):
                                p = pv[vc % 2]
                                v = vst[vc % NVST]
                                t0 = r + d * 128 * j
                                for c in range(8):
                                    S.op("pe", lambda e, c=c, p=p, t0=t0, d=d, g=g: e.matmul(p[:, 0:128], lhsT=hT[:, c, t0:t0 + d * 127 + 1:d],
                                                                                             rhs=wv[:, c, 384 + g * 128:384 + (g + 1) * 128], start=(c == 0), stop=(c == 7)),
                                         R=["wv"] + hkeys, W=[f"pv{vc % 2}"], inc=(c == 7))
                                dst = v[:, 0, :].rearrange("p (b d) -> p b d", d=64)[:, 0:4:3, :]
                                src = p[:, 0:128].rearrange("p (h d) -> p h d", h=2)
                                S.op("dve" if vc % 2 else "act",
                                     (lambda e, dst=dst, src=src: e.tensor_copy(out=dst, in_=src)) if vc % 2 else
                                     (lambda e, dst=dst, src=src: e.activation(out=dst, in_=src, func=AF.Copy)),
                                     R=[f"pv{vc % 2}"], W=[f"vst{vc % NVST}"])
                                row0 = (r * ntc + j) * 128
                                S.dma("pool" if vc % 2 else "sp", f"stv{vc % NVST}", vp_d[3 + g, row0:row0 + 128, :], v[:, 0, :], R=[f"vst{vc % NVST}"], W=[f"vpd{vc}"])
                                vc += 1

                    chk(l, "A3")
                    S.op("pool", lambda e: e.memset(vmp[:], 0.0), W=["vmp"])
                    for pr in range(2):
                        p = pp[pr]
                        for c in range(8):
                            S.op("pe", lambda e, c=c, p=p, pr=pr: e.matmul(p[:, 0:NMEM], lhsT=wkv[:, c, pr * 128:(pr + 1) * 128], rhs=mnT[:, c, :],
                                                                            start=(c == 0), stop=(c == 7)),
                                 R=["wkv", "mnT"], W=[f"pp{pr}"], inc=(c == 7))
                        S.op("act", lambda e, p=p, pr=pr: e.activation(out=kmT[:, pr, :], in_=p[:, 0:NMEM], func=AF.Copy), R=[f"pp{pr}"], W=["kmT"])
                    for kt in range(2):
                        p = pv[kt]
                        for c in range(8):
                            S.op("pe", lambda e, c=c, p=p, kt=kt: e.matmul(p[:, 0:256], lhsT=mnT[:, c, kt * 128:(kt + 1) * 128], rhs=wkv[:, c, 256:512],
                                                                            start=(c == 0), stop=(c == 7)),
                                 R=["wkv", "mnT"], W=[f"pv{kt}"], inc=(c == 7))
                        dst = vmp[:, kt, :, :].rearrange("p a (b d) -> p a b d", d=64)[:, :, 0:4:3, :]
                        src = p[:, 0:256].rearrange("p (a h d) -> p a h d", a=2, h=2)
                        S.op("dve", lambda e, dst=dst, src=src: e.tensor_copy(out=dst, in_=src), R=[f"pv{kt}"], W=["vmp"])
                    S.barrier()

                if not on(l, "C"):
                    break
                with ExitStack() as st:
                    qz = [sb(st, f"qz{i}", [128, 2, T], BF16) for i in range(2)]
                    kT = [sb(st, f"kT{i}", [128, T], BF16) for i in range(2)]
                    vp = [sb(st, f"vp{i}", [128, NT, 256], BF16) for i in range(2)]
                    for i in range(2):
                        S.op("pool", lambda e, i=i: e.memset(qz[i][:], 0.0), W=[f"qT{i}"])
                    nabb = [sb(st, f"nabb{i}", [128, 3, 2, 640], BF16) for i in range(2)]
                    numB = sb(st, "numB", [128, T], F32)
                    denB = sb(st, "denB", [128, T], F32)
                    S.op("pool", lambda e: e.memset(numB[:], 0.0), W=["numB"])
                    S.op("pool", lambda e: e.memset(denB[:], 1.0), W=["denB"])
                    ost = [sb(st, f"ost{i}", [128, T], BF16) for i in range(2)]
                    pT = [sb(st, f"pT{i}", [128, 1536], BF16) for i in range(2)]
                    rec = [sb(st, f"rec{i}", [128, 256], F32) for i in range(2)]
                    pS = [ps(st, f"pS{i}", [128, 1536]) for i in range(2)]
                    pOD = [ps(st, f"pOD{i}", [128, 512]) for i in range(2)]

                    jobs = [("A", 0, 3, 0), ("A", 1, 4, 1), ("A", 2, 5, 2), ("B", 6, 9, 3), ("B", 7, 10, 4), ("B", 8, 11, 5),
                            ("M", 12, None, None), ("M", 13, None, None)]

                    def load_job(ji):
                        kind, qi, ki, vi = jobs[ji]
                        s = ji % 2
                        nq_ = nb_of[qi] * 512
                        S.dma("sp", f"ldq{s}", qz[s][0:64, 0, 0:nq_], qk_d[qi, 0:64, 0:nq_], W=[f"qT{s}"])
                        S.dma("sp", f"ldq{s}", qz[s][64:128, 1, 0:nq_], qk_d[qi, 64:128, 0:nq_], W=[f"qT{s}"])
                        if ki is not None:
                            nk_ = nb_of[ki] * 512
                            S.dma("sp", f"ldq{s}", kT[s][:, 0:nk_], qk_d[ki, :, 0:nk_], W=[f"kT{s}"])
                            if kind == "A":
                                S.dma("sp", f"ldq{s}", vp[s][:, 0:nva, :], vp_d[vi, 0:nva * 128, :].rearrange("(t p) x -> p t x", p=128), W=[f"vp{s}"])
                            else:
                                g_ = vi - 3
                                d_ = DIL[g_]
                                ntc_ = NT // d_
                                nt_ = kmaxs[g_] + 1
                                if nt_ == ntc_:
                                    S.dma("sp", f"ldq{s}", vp[s][:], vp_d[vi].rearrange("(t p) x -> p t x", p=128), W=[f"vp{s}"])
                                else:
                                    for r_ in range(d_):
                                        S.dma("sp", f"ldq{s}", vp[s][:, r_ * ntc_:r_ * ntc_ + nt_, :],
                                              vp_d[vi, r_ * ntc_ * 128:(r_ * ntc_ + nt_) * 128, :].rearrange("(t p) x -> p t x", p=128), W=[f"vp{s}"])
                        if kind == "A":
                            for v_ in range(3):
                                S.dma("pool", f"ldnab{s}", nabb[s][:, v_, :, :], nab_d[l, v_, 2 * ji:2 * ji + 2, :, :].rearrange("h p x -> p h x"),
                                      W=[f"nabb{s}"])

                    MM = dict(skip_group_check=True)
                    tiles = []

                    def add_tile(ji, units, bias_pieces, nd_list, ncols, out_cb, first=False):
                        st_ = {}

                        def stage0():
                            if first and ji + 1 < len(jobs):
                                load_job(ji + 1)
                            b = st_["b"] = hc[0] % 2
                            hc[0] += 1
                            started = set()

                            def flag(c0):
                                bk = c0 // 512
                                f = bk not in started
                                started.add(bk)
                                return f
                            pieces = []
                            for (c0, n, ap, key) in bias_pieces:
                                o = 0
                                while o < n:
                                    m = min(n - o, 512 - (c0 + o) % 512)
                                    pieces.append((c0 + o, m, ap[:, o:o + m], key))
                                    o += m
                            for (c0, n, ap, key) in pieces:
                                S.op("pe", lambda e: e.matmul(pS[b][:, c0:c0 + n], lhsT=ident[:], rhs=ap, start=flag(c0), stop=False, **MM),
                                     R=["ident", key], W=[f"pS{b}"], inc=False)
                            for ui, (c0, n, la, lk, ra, rk) in enumerate(units):
                                S.op("pe", lambda e: e.matmul(pS[b][:, c0:c0 + n], lhsT=la, rhs=ra, start=flag(c0), stop=True, **MM),
                                     R=[lk, rk], W=[f"pS{b}"], inc=(ui == len(units) - 1))
                            if _os.environ.get("ATT_EXP", "1") == "1":
                                S.op("act", lambda e: e.activation(out=pT[b][:, 0:ncols], in_=pS[b][:, 0:ncols], func=AF.Exp),
                                     R=[f"pS{b}"], W=[f"pT{b}"])

                        def stage1():
                            b = st_["b"]
                            ob = tc[0] % 2
                            tc[0] += 1
                            w = nd_list[0][1]
                            for ni, (c0, n, va, vk, hh) in enumerate(nd_list):
                                S.op("pe", lambda e: e.matmul(pOD[ob][:, 0:w], lhsT=va, rhs=pT[b][:, c0:c0 + n], start=(ni == 0), stop=False, **MM),
                                     R=[vk, f"pT{b}"], W=[f"pOD{ob}"], inc=False)
                            for ni, (c0, n, va, vk, hh) in enumerate(nd_list):
                                S.op("pe", lambda e: e.matmul(pOD[ob][:, w:2 * w], lhsT=onesh[:, hh, :], rhs=pT[b][:, c0:c0 + n], start=False,
                                                              stop=(ni == len(nd_list) - 1), **MM),
                                     R=["onesh", f"pT{b}"], W=[f"pOD{ob}"], inc=(ni == len(nd_list) - 1))
                            out_cb(ob, w)
                        tiles.append((stage0, stage1))

                    hc = [0]
                    tc = [0]

                    def out_norm(osb, qsl):
                        def cbk(ob, w):
                            r = rec[ob]
                            S.op("dve", lambda e: e.reciprocal(out=r[:, 0:w], in_=pOD[ob][:, w:2 * w]), R=[f"pOD{ob}"], W=[f"rec{ob}"])
                            S.op("dve", lambda e: e.tensor_tensor(out=ost[osb][:, qsl], in0=pOD[ob][:, 0:w], in1=r[:, 0:w], op=ALU.mult),
                                 R=[f"pOD{ob}", f"rec{ob}"], W=[f"ost{osb}"])
                        return cbk

                    def out_acc(first, qsl):
                        def cbk(ob, w):
                            if first:
                                S.op("dve", lambda e: e.tensor_copy(out=numB[:, qsl], in_=pOD[ob][:, 0:w]), R=[f"pOD{ob}"], W=["numB"])
                                S.op("dve", lambda e: e.tensor_copy(out=denB[:, qsl], in_=pOD[ob][:, w:2 * w]), R=[f"pOD{ob}"], W=["denB"])
                            else:
                                S.op("dve", lambda e: e.tensor_tensor(out=numB[:, qsl], in0=pOD[ob][:, 0:w], in1=numB[:, qsl], op=ALU.add),
                                     R=[f"pOD{ob}", "numB"], W=["numB"])
                                S.op("dve", lambda e: e.tensor_tensor(out=denB[:, qsl], in0=pOD[ob][:, w:2 * w], in1=denB[:, qsl], op=ALU.add),
                                     R=[f"pOD{ob}", "denB"], W=["denB"])
                        return cbk

                    def with_tail(cbk, tail):
                        def f(ob, w):
                            cbk(ob, w)
                            tail()
                        return f

                    def std_tile(ji, s, ksl, kTt, qsl, bias_pieces, vfn, out_cb, first):
                        nk = len(ksl)
                        W_ = nk * 128
                        units = []
                        nd = []
                        for hh in range(2):
                            rows = slice(hh * 64, (hh + 1) * 64)
                            for jj, ks in enumerate(ksl):
                                units.append((hh * W_ + jj * 128, 128, kTt[0][:, ks], kTt[1], qz[s][:, hh, qsl], f"qT{s}"))
                                va = vfn(jj, hh)
                                nd.append((hh * W_ + jj * 128, 128, va[0], va[1], hh))
                        add_tile(ji, units, bias_pieces, nd, 2 * W_, out_cb, first)

                    load_job(0)
                    oc = 0
                    import os as _os
                    _only = _os.environ.get("ATT_ONLY", "ABM")
                    for ji, (kind, qi, ki, vi) in enumerate(jobs):
                        s = ji % 2
                        if kind not in _only:
                            continue
                        if kind == "A":
                            pr = ji
                            osb = oc % 2
                            oc += 1
                            nm_ = TFL[l] // 128
                            for m in range(nm_):
                                kt0 = min(max(m - 2, 0), NT - 5)
                                var = {0: 0, 1: 1}.get(m, 2)
                                ksl = [slice((kt0 + j) * 128, (kt0 + j + 1) * 128) for j in range(5)]
                                qsl = slice(m * 128, (m + 1) * 128)
                                bp = [(0, 1280, nabb[s][:, var, :, :].rearrange("p h x -> p (h x)"), f"nabb{s}")]
                                cbk = out_norm(osb, qsl)
                                if m == nm_ - 1:
                                    cbk = with_tail(cbk, lambda pr=pr, osb=osb: S.dma("pool", "sto", o_d[:, pr, 0:TFL[l]], ost[osb][:, 0:TFL[l]], R=[f"ost{osb}"], W=[f"od:{pr}"]))
                                std_tile(ji, s, ksl, (kT[s], f"kT{s}"), qsl, bp,
                                         lambda jj, hh, kt0=kt0, s=s: (vp[s][:, kt0 + jj, hh * 128:(hh + 1) * 128], f"vp{s}"), cbk, m == 1)
                        elif kind == "B":
                            g = ji - 3
                            d = DIL[g]
                            ntc = NT // d
                            njq = min(ntc, -(-TFL[l] // (128 * d)))
                            for r in range(d):
                                for j in range(njq):
                                    jl = [jj for jj in (j - 1, j, j + 1) if 0 <= jj < ntc]
                                    ksl = [slice(r + d * 128 * jj, r + d * 128 * jj + d * 127 + 1, d) for jj in jl]
                                    qsl = slice(r + d * 128 * j, r + d * 128 * j + d * 127 + 1, d)
                                    b0 = (jl[0] - (j - 1)) * 128
                                    W_ = len(jl) * 128
                                    if W_ == 384:
                                        bp = [(0, 768, dilbb[:, 2 * g:2 * g + 2, :].rearrange("p h x -> p (h x)"), "dilbb")]
                                    else:
                                        bp = [(hh * W_, W_, dilbb[:, 2 * g + hh, b0:b0 + W_], "dilbb") for hh in range(2)]
                                    cbk = out_acc(g == 0, qsl)
                                    if g == 2 and r == d - 1 and j == njq - 1:
                                        osb = oc % 2
                                        oc += 1

                                        def tailB(osb=osb):
                                            TF_ = TFL[l]
                                            S.op("dve", lambda e: e.reciprocal(out=denB[:, 0:TF_], in_=denB[:, 0:TF_]), R=["denB"], W=["denB"])
                                            S.op("dve", lambda e: e.tensor_tensor(out=ost[osb][:, 0:TF_], in0=numB[:, 0:TF_], in1=denB[:, 0:TF_], op=ALU.mult),
                                                 R=["numB", "denB"], W=[f"ost{osb}"])
                                            S.dma("pool", "sto", o_d[:, 3, 0:TF_], ost[osb][:, 0:TF_], R=[f"ost{osb}"], W=["od:3"])
                                        cbk = with_tail(cbk, tailB)
                                    std_tile(ji, s, ksl, (kT[s], f"kT{s}"), qsl, bp,
                                             lambda jj, hh, jl=jl, r=r, ntc=ntc, s=s: (vp[s][:, r * ntc + jl[jj], hh * 128:(hh + 1) * 128], f"vp{s}"),
                                             cbk, r == 0 and j == 1)
                        else:
                            pr = ji - 6
                            osb = oc % 2
                            oc += 1
                            qb = [(q0, min(256, TFL[l] - q0)) for q0 in range(0, TFL[l], 256)]
                            for bi, (q0, w) in enumerate(qb):
                                qsl = slice(q0, q0 + w)
                                units = []
                                nd = []
                                for hh in range(2):
                                    rows = slice(hh * 64, (hh + 1) * 64)
                                    for kt in range(2):
                                        c0 = (hh * 2 + kt) * w
                                        units.append((c0, w, kmT[:, pr, kt * 128:(kt + 1) * 128], "kmT", qz[s][:, hh, qsl], f"qT{s}"))
                                        nd.append((c0, w, vmp[:, kt, pr, hh * 128:(hh + 1) * 128], "vmp", hh))
                                cbk = out_norm(osb, qsl)
                                if bi == len(qb) - 1:
                                    cbk = with_tail(cbk, lambda pr=pr, osb=osb: S.dma("pool", "sto", o_d[:, 4 + pr, 0:TFL[l]], ost[osb][:, 0:TFL[l]], R=[f"ost{osb}"], W=[f"od:{4 + pr}"]))
                                add_tile(ji, units, [], nd, 4 * w, cbk, bi == 1)
                    for g_ in range(len(tiles) + 1):
                        if g_ < len(tiles):
                            tiles[g_][0]()
                        if g_ >= 1 and _os.environ.get("ATT_ST1", "1") == "1":
                            tiles[g_ - 1][1]()
                    S.barrier()

                def post_head(st_bufs, emit_y, sz):
                    yT, ysq, sd2, pY, pSS = st_bufs
                    for j2 in range(8):
                        p = pY[j2 % 2]
                        emit_y(j2, p, f"pY{j2 % 2}")
                        S.op("act", lambda e, p=p, j2=j2: e.activation(out=yT[:, j2, 0:sz], in_=p[:, 0:sz], func=AF.Copy), R=[f"pY{j2 % 2}"], W=[f"yT:{j2}"])
                        S.op("act", lambda e, p=p, j2=j2: e.activation(out=ysq[:, j2, 0:sz], in_=p[:, 0:sz], func=AF.Square), R=[f"pY{j2 % 2}"], W=[f"ysq:{j2}"])
                    for c in range(8):
                        S.op("pe", lambda e, c=c: e.matmul(pSS[:, 0:sz], lhsT=ones[:], rhs=ysq[:, c, 0:sz], start=(c == 0), stop=(c == 7)),
                             R=[f"ysq:{c}", "ones"], W=["pSS"], inc=(c == 7))
                    S.op("act", lambda e: e.activation(out=sd2[:, 0:sz], in_=pSS[:, 0:sz], func=AF.Sqrt, scale=1.0 / D, bias=eps[:]), R=["pSS", "eps"], W=["sd2"])

                def post_tail(st_bufs, sz, gcol, xb_t, xkey, dst_ap):
                    yT, ysq, sd2, pY, pSS = st_bufs
                    pieces = [lambda: S.op("dve", lambda e: e.reciprocal(out=sd2[:, 0:sz], in_=sd2[:, 0:sz]), R=["sd2"], W=["sd2"])]

                    def piece(j2):
                        S.op("dve", lambda e: e.scalar_tensor_tensor(out=yT[:, j2, 0:sz], in0=yT[:, j2, 0:sz], scalar=cv[:, gcol + j2:gcol + j2 + 1],
                                                                      in1=sd2[:, 0:sz], op0=ALU.mult, op1=ALU.mult), R=[f"yT:{j2}", "sd2", "cv"], W=[f"yT:{j2}"])
                        S.op("pool", lambda e: e.tensor_tensor(out=xb_t[:, j2, 0:sz], in0=yT[:, j2, 0:sz], in1=xb_t[:, j2, 0:sz], op=ALU.add),
                             R=[f"yT:{j2}", xkey], W=[f"{xkey}:{j2}"])
                    for j2 in range(8):
                        pieces.append(lambda j2=j2: piece(j2))
                    pieces.append(lambda: S.dma("pool", "stx", dst_ap, xb_t[:, :, 0:sz], R=[xkey] + [f"{xkey}:{j2}" for j2 in range(8)], W=["xdst"]))
                    return pieces

                def post_block(st_bufs, emit_y, sz, gcol, xb_t, xkey, dst_ap):
                    post_head(st_bufs, emit_y, sz)
                    for pc_ in post_tail(st_bufs, sz, gcol, xb_t, xkey, dst_ap):
                        pc_()

                blks = blocks_of(TFL[l])
                nblk = len(blks)

                if not on(l, "D"):
                    break
                with ExitStack() as st:
                    wg = sb(st, "wg", [128, 8, 3072], BF16)
                    wbr = sb(st, "wbr", [128, 6, D], BF16)
                    wo = sb(st, "wo", [128, 8, D], BF16)
                    hb = [sb(st, f"hb{i}", [128, 8, 512], BF16) for i in range(2)]
                    ob_ = [sb(st, f"ob{i}", [128, 6, 512], BF16) for i in range(2)]
                    xbufD = sb(st, "xbufD", [128, 8, 512], F32)
                    sig = [sb(st, f"sig{i}", [128, 512], F32) for i in range(2)]
                    accs = [sb(st, f"acc{i}", [128, 512], F32) for i in range(2)]
                    tq = [sb(st, f"tq{i}", [128, 512], F32) for i in range(6)]
                    mT = sb(st, "mT", [128, 8, 512], BF16)
                    yT = sb(st, "yT", [128, 8, 512], F32)
                    ysq = sb(st, "ysq", [128, 8, 512], BF16)
                    sd2 = sb(st, "sd2", [128, 512], F32)
                    pG = [ps(st, f"pG{i}", [128, 512]) for i in range(2)]
                    pP = [ps(st, f"pP{i}", [128, 512]) for i in range(2)]
                    pY = [ps(st, f"pY{i}", [128, 512]) for i in range(2)]
                    pSS = ps(st, "pSS", [128, 512])
                    win = w_in_d[l].rearrange("(c p) n -> p c n", p=128)
                    for hq in range(2):
                        for br in range(3):
                            c0_ = br * 1024 + hq * 512
                            S.dma("pool", f"wg{hq}", wg[:, :, c0_:c0_ + 512], win[:, :, 2560 + c0_:2560 + c0_ + 512], W=[f"wg{hq}"])
                        if hq == 0:
                            S.dma("pool", "wbr", wbr[:, 0:3, :], w_bra_d[l].rearrange("(c p) n -> p c n", p=128), W=["wbr"])
                            S.dma("pool", "wbr", wbr[:, 3:4, :], w_brb_d[l].rearrange("(c p) n -> p c n", p=128), W=["wbr"])
                            S.dma("pool", "wbr", wbr[:, 4:6, :], w_brm_d[l].rearrange("(c p) n -> p c n", p=128), W=["wbr"])
                    S.dma("pool", "wo", wo[:], w_out_d[l].rearrange("(c p) n -> p c n", p=128), W=["wo"])
                    brk = [(0, 3), (3, 4), (4, 6)]

                    def load_blk(n):
                        s = n % 2
                        t0, sz = blks[n]
                        S.dma("sp", f"ldh{s}", hb[s][:, :, 0:sz], xview(hT_d, t0, sz), W=[f"hb{s}"])
                        S.dma("sp", f"ldh{s}", ob_[s][:, :, 0:sz], o_d[:, :, t0:t0 + sz], W=[f"ob{s}"])
                    load_blk(0)
                    gc = 0
                    pend = []
                    S.dma("sp", "ldxD", xbufD[:, :, 0:blks[0][1]], xview(xsrc, *blks[0]), W=["xbufD"])
                    for n in range(nblk):
                        s = n % 2
                        t0, sz = blks[n]
                        if n + 1 < nblk:
                            load_blk(n + 1)
                        for j in range(8):
                            for br in range(3):
                                if pend and (j * 3 + br) % 2 == 1:
                                    pend.pop(0)()
                                g_ = pG[gc % 2]
                                p_ = pP[gc % 2]
                                sg_ = sig[gc % 2]
                                for c in range(8):
                                    S.op("pe", lambda e, c=c, g_=g_, br=br, j=j: e.matmul(g_[:, 0:sz], lhsT=wg[:, c, br * 1024 + j * 128:br * 1024 + (j + 1) * 128],
                                                                                          rhs=hb[s][:, c, 0:sz], start=(c == 0), stop=(c == 7)),
                                         R=[f"wg{j // 4}", f"hb{s}"], W=[f"pG{gc % 2}"], inc=(c == 7))
                                k0, k1 = brk[br]
                                for k in range(k0, k1):
                                    S.op("pe", lambda e, k=k, p_=p_, j=j: e.matmul(p_[:, 0:sz], lhsT=wbr[:, k, j * 128:(j + 1) * 128], rhs=ob_[s][:, k, 0:sz],
                                                                                    start=(k == k0), stop=(k == k1 - 1)),
                                         R=["wbr", f"ob{s}"], W=[f"pP{gc % 2}"], inc=(k == k1 - 1))
                                bcol = cb + 32 + br * 8 + j
                                S.op("act", lambda e, g_=g_, sg_=sg_, bcol=bcol: e.activation(out=sg_[:, 0:sz], in_=g_[:, 0:sz], func=AF.Sigmoid, bias=cv[:, bcol:bcol + 1]),
                                     R=[f"pG{gc % 2}", "cv"], W=[f"sig{gc % 2}"])
                                tq_ = tq[(j % 2) * 3 + br]
                                tqk = f"tq{(j % 2) * 3 + br}"
                                S.op("dve", lambda e, sg_=sg_, p_=p_, tq_=tq_: e.tensor_tensor(out=tq_[:, 0:sz], in0=p_[:, 0:sz], in1=sg_[:, 0:sz], op=ALU.mult),
                                     R=[f"pP{gc % 2}", f"sig{gc % 2}"], W=[tqk])
                                if br == 2:
                                    jb = (j % 2) * 3
                                    ac_ = accs[j % 2]
                                    S.op("pool", lambda e, jb=jb, ac_=ac_: e.tensor_tensor(out=ac_[:, 0:sz], in0=tq[jb][:, 0:sz], in1=tq[jb + 1][:, 0:sz], op=ALU.add),
                                         R=[f"tq{jb}", f"tq{jb + 1}"], W=[f"acc{j % 2}"])
                                    S.op("pool", lambda e, jb=jb, ac_=ac_, j=j: e.tensor_tensor(out=mT[:, j, 0:sz], in0=ac_[:, 0:sz], in1=tq[jb + 2][:, 0:sz], op=ALU.add),
                                         R=[f"acc{j % 2}", f"tq{jb + 2}"], W=["mT"])
                                gc += 1

                        def emit_y(j2, p, pkey):
                            for j in range(8):
                                S.op("pe", lambda e, j=j: e.matmul(p[:, 0:sz], lhsT=wo[:, j, j2 * 128:(j2 + 1) * 128], rhs=mT[:, j, 0:sz], start=(j == 0), stop=(j == 7)),
                                     R=["wo", "mT"], W=[pkey], inc=(j == 7))
                        while pend:
                            pend.pop(0)()
                        post_head((yT, ysq, sd2, pY, pSS), emit_y, sz)
                        pend = post_tail((yT, ysq, sd2, pY, pSS), sz, cb + 8, xbufD, "xbufD", xview(xmid, t0, sz))
                        if n + 1 < nblk:
                            t1_, sz1_ = blks[n + 1]
                            pend.append(lambda t1_=t1_, sz1_=sz1_: S.dma("sp", "ldxD", xbufD[:, :, 0:sz1_], xview(xsrc, t1_, sz1_), W=["xbufD"]))
                    while pend:
                        pend.pop(0)()
                    S.barrier()

                if not on(l, "E"):
                    break
                TE = TFL[l]
                fblks = blks if l == 0 else blks[:OWN // 512]
                with ExitStack() as st:
                    hT = sb(st, "h2T", [128, 8, TE], BF16)
                    with ExitStack() as st2:
                        xbuf = [sb(st2, f"xbufE{i}", [128, 8, 512], F32) for i in range(3)]
                        nbufs = dict(sq=[sb(st2, f"sqE{i}", [128, 8, 512], BF16) for i in range(2)],
                                     sd=[sb(st2, f"sdE{i}", [128, 512], F32) for i in range(2)],
                                     tp=[sb(st2, f"tpE{i}", [128, 3, 512], F32) for i in range(2)],
                                     pss=[ps(st2, f"pssE{i}", [128, 512]) for i in range(2)])
                        S.dma("sp", "ldx0", xbuf[0][:, :, 0:blks[0][1]], xview(xmid, *blks[0]), W=["xbufE0"])
                        for n in range(nblk + 1):
                            if n + 1 < nblk:
                                S.dma("sp", f"ldx{(n + 1) % 3}", xbuf[(n + 1) % 3][:, :, 0:blks[n + 1][1]], xview(xmid, *blks[n + 1]), W=[f"xbufE{(n + 1) % 3}"])
                            if n < nblk:
                                t0, sz = blks[n]
                                norm_p1(xbuf[n % 3][:, :, 0:sz], f"xbufE{n % 3}", sz, nbufs, n)
                            if n >= 1:
                                m_ = n - 1
                                t0, sz = blks[m_]
                                norm_p2(xbuf[m_ % 3][:, :, 0:sz], f"xbufE{m_ % 3}", sz, cb + 16, hT[:, :, t0:t0 + sz], f"h2T:{m_}", nbufs, m_)
                            if n < nblk:
                                norm_p1b(blks[n][1], nbufs, n)
                        S.barrier()
                    pA = [ps(st, f"pA{i}", [128, 512]) for i in range(2)]
                    pB = [ps(st, f"pB{i}", [128, 512]) for i in range(2)]
                    wa = [sb(st, f"wa{i}", [128, 8, 256], BF16) for i in range(2)]
                    wb_ = [sb(st, f"wb_{i}", [128, 8, 256], BF16) for i in range(2)]
                    Ua = [sb(st, f"Ua{i}", [128, TE + 2], F32) for i in range(2)]
                    Ub = [sb(st, f"Ub{i}", [128, TE + 2], F32) for i in range(2)]
                    nfb = len(fblks)
                    ca = [sb(st, f"ca{i}", [128, 512], F32) for i in range(nfb)]
                    cbt = [sb(st, f"cbt{i}", [128, 512], F32) for i in range(nfb)]
                    fst = [sb(st, f"fst{i}", [128, TE], BF16) for i in range(2)]
                    wup = w_up_d[l].rearrange("(c p) n -> p c n", p=128)
                    TFF = fblks[-1][0] + fblks[-1][1]

                    def load_w(gi):
                        s = gi % 2
                        S.dma("pool", f"wa{s}", wa[s][:], wup[:, :, gi * 256:(gi + 1) * 256], W=[f"wa{s}"])
                        S.dma("pool", f"wa{s}", wb_[s][:], wup[:, :, DFF + gi * 256:DFF + (gi + 1) * 256], W=[f"wb_{s}"])
                    load_w(0)
                    load_w(1)
                    for i in range(2):
                        for U, nm in ((Ua, "Ua"), (Ub, "Ub")):
                            S.op("pool", lambda e, U=U, i=i: e.memset(U[i][:, 0:1], 0.0), W=[f"{nm}{i}"])
                            S.op("pool", lambda e, U=U, i=i: e.memset(U[i][:, TE + 1:TE + 2], 0.0), W=[f"{nm}{i}"])
                    cw = cb + 56
                    cbias = cb + 188
                    units = [(jp, n) for jp in range(22) for n in range(nblk)]
                    pcs = [0]

                    def stage0(jp, n):
                        gi, jj = jp // 2, jp % 2
                        s, u = gi % 2, jp % 2
                        if n == 0 and jj == 0 and gi >= 1 and gi + 1 < 11:
                            load_w(gi + 1)
                        t0, sz = blks[n]
                        pc = pcs[0]
                        pcs[0] += 1
                        a_, b_ = pA[pc % 2], pB[pc % 2]
                        for (p_, w_, wkey, pkey) in ((a_, wa[s], f"wa{s}", f"pA{pc % 2}"), (b_, wb_[s], f"wb_{s}", f"pB{pc % 2}")):
                            for c in range(8):
                                S.op("pe", lambda e, c=c, p_=p_, w_=w_: e.matmul(p_[:, 0:sz], lhsT=w_[:, c, jj * 128:(jj + 1) * 128], rhs=hT[:, c, t0:t0 + sz],
                                                                                 start=(c == 0), stop=(c == 7)),
                                     R=[wkey, f"h2T:{n}"], W=[pkey], inc=(c == 7))
                        S.op("act", lambda e: e.activation(out=Ua[u][:, 1 + t0:1 + t0 + sz], in_=a_[:, 0:sz], func=AF.Copy),
                             R=[f"pA{pc % 2}"], W=[f"Ua{u}:{n}"])
                        S.op("act", lambda e: e.activation(out=Ub[u][:, 1 + t0:1 + t0 + sz], in_=b_[:, 0:sz], func=AF.Copy),
                             R=[f"pB{pc % 2}"], W=[f"Ub{u}:{n}"])

                    def ukeys(nm, u, n):
                        return [f"{nm}{u}:{k}" for k in (n - 1, n, n + 1) if 0 <= k < nblk] + [f"{nm}{u}"]

                    def stage1(jp, n):
                        if n >= nfb:
                            return
                        u = jp % 2
                        base, sz = fblks[n]
                        for (U, nm, dstt, dkey, ch) in ((Ua, "Ua", ca, "ca", jp), (Ub, "Ub", cbt, "cbt", 22 + jp)):
                            w0 = cv[:, cw + ch:cw + ch + 1]
                            w1 = cv[:, cw + 44 + ch:cw + 44 + ch + 1]
                            w2 = cv[:, cw + 88 + ch:cw + 88 + ch + 1]
                            bb = cv[:, cbias + ch:cbias + ch + 1]
                            o = dstt[n]
                            S.op("act", lambda e, o=o, U=U, w1=w1, bb=bb: e.activation(out=o[:, 0:sz], in_=U[u][:, base + 1:base + 1 + sz], func=AF.Identity,
                                                                                        scale=w1, bias=bb),
                                 R=ukeys(nm, u, n) + ["cv"], W=[f"{dkey}{n}"])
                            S.op("dve", lambda e, o=o, U=U, w0=w0: e.scalar_tensor_tensor(out=o[:, 0:sz], in0=U[u][:, base:base + sz], scalar=w0, in1=o[:, 0:sz],
                                                                                          op0=ALU.mult, op1=ALU.add),
                                 R=ukeys(nm, u, n) + ["cv", f"{dkey}{n}"], W=[f"{dkey}{n}"])
                            S.op("dve", lambda e, o=o, U=U, w2=w2: e.scalar_tensor_tensor(out=o[:, 0:sz], in0=U[u][:, base + 2:base + 2 + sz], scalar=w2, in1=o[:, 0:sz],
                                                                                          op0=ALU.mult, op1=ALU.add),
                                 R=ukeys(nm, u, n) + ["cv", f"{dkey}{n}"], W=[f"{dkey}{n}"])

                    def stage2(jp, n):
                        if n >= nfb:
                            return
                        sz = fblks[n][1]
                        S.op("act", lambda e: e.activation(out=ca[n][:, 0:sz], in_=ca[n][:, 0:sz], func=AF.Gelu_apprx_tanh), R=[f"ca{n}"], W=[f"ca{n}"])

                    def stage3(jp, n):
                        if n >= nfb:
                            return
                        u = jp % 2
                        base, sz = fblks[n]
                        S.op("pool", lambda e: e.tensor_tensor(out=fst[u][:, base:base + sz], in0=ca[n][:, 0:sz], in1=cbt[n][:, 0:sz], op=ALU.mult),
                             R=[f"ca{n}", f"cbt{n}"], W=[f"fst{u}"])
                        if n == nfb - 1:
                            S.dma("pool", "stf", f_d[:, jp, 0:TFF], fst[u][:, 0:TFF], R=[f"fst{u}"], W=[f"fd:{jp}"])

                    lags = (0, nblk, nblk + 2, nblk + 3)
                    stages = (stage0, stage1, stage2, stage3)
                    for g in range(len(units) + lags[-1]):
                        for lag, fn in zip(lags, stages):
                            k = g - lag
                            if 0 <= k < len(units):
                                fn(*units[k])
                    S.barrier()

                if not on(l, "F"):
                    break
                with ExitStack() as st:
                    wd = sb(st, "wd", [128, 22, D], BF16)
                    fb = [sb(st, f"fb{i}", [128, 22, 512], BF16) for i in range(2)]
                    xbuf = [sb(st, f"xbufF{i}", [128, 8, 512], F32) for i in range(2)]
                    yT = sb(st, "yTF", [128, 8, 512], F32)
                    ysq = sb(st, "ysqF", [128, 8, 512], BF16)
                    sd2 = sb(st, "sd2F", [128, 512], F32)
                    pY = [ps(st, f"pYF{i}", [128, 512]) for i in range(2)]
                    pSS = ps(st, "pSSF", [128, 512])
                    wdn = w_dn_d[l].rearrange("(c p) n -> p c n", p=128)
                    for h in range(4):
                        S.dma("pool", f"wd{h}", wd[:, :, h * 256:(h + 1) * 256], wdn[:, :, h * 256:(h + 1) * 256], W=[f"wd{h}"])

                    def load_blkF(n):
                        s = n % 2
                        t0, sz = fblks[n]
                        S.dma("sp", f"ldf{s}", fb[s][:, :, 0:sz], f_d[:, :, t0:t0 + sz], W=[f"fb{s}"])
                        S.dma("sp", f"ldf{s}", xbuf[s][:, :, 0:sz], xview(xmid, t0, sz), W=[f"xbufF{s}"])
                    load_blkF(0)
                    for n in range(len(fblks)):
                        s = n % 2
                        t0, sz = fblks[n]
                        if n + 1 < len(fblks):
                            load_blkF(n + 1)

                        def emit_y(j2, p, pkey):
                            for j in range(22):
                                S.op("pe", lambda e, j=j: e.matmul(p[:, 0:sz], lhsT=wd[:, j, j2 * 128:(j2 + 1) * 128], rhs=fb[s][:, j, 0:sz], start=(j == 0), stop=(j == 21)),
                                     R=[f"wd{j2 // 2}", f"fb{s}"], W=[pkey], inc=(j == 21))
                        post_block((yT, ysq, sd2, pY, pSS), emit_y, sz, cb + 24, xbuf[s], f"xbufF{s}", xview(xdst, t0, sz))
                    S.barrier()
        except _Stop:
            pass
        S.barrier()
    return nc


def _colvec(v):
    return np.ascontiguousarray(np.asarray(v, np.float32).reshape(-1, 128).T)


def _na_tables(rpb, rev):
    out = np.full((5, 6, 128, 640), NEG, np.float32)
    for v, (m, kt0) in enumerate([(0, 0), (1, 0), (2, 0)]):
        qtok = m * 128 + np.arange(128)
        ktok = kt0 * 128 + np.arange(640)
        if rev:
            qtok = SEQ - 1 - qtok
            ktok = SEQ - 1 - ktok
        i, c = qtok // 64, qtok % 64
        kr, kc = ktok // 64, ktok % 64
        r0 = np.clip(i - 4, 0, 56)
        c0 = np.clip(c - 8, 0, 48)
        valid = ((kr[:, None] >= r0[None]) & (kr[:, None] < r0[None] + 8) &
                 (kc[:, None] >= c0[None]) & (kc[:, None] < c0[None] + 16))
        ri = np.clip(kr[:, None] - i[None] + 7, 0, 14)
        ci = np.clip(kc[:, None] - c[None] + 15, 0, 30)
        tab = np.where(valid[None], rpb[:, ri, ci], np.float32(NEG)).astype(np.float32)
        out[v] = tab.reshape(6, 5, 128, 128).transpose(0, 2, 1, 3).reshape(6, 128, 640)
    return out


def _dil_tables():
    slopes = 2.0 ** (-8.0 * np.arange(1, 7, dtype=np.float32) / 6)
    kp = np.arange(128)[:, None, None]
    j = np.arange(3)[None, :, None]
    qp = np.arange(128)[None, None, :]
    delta = (j - 1) * 128 + kp - qp
    tabs = np.empty((128, 6, 384), np.float32)
    for h in range(6):
        d = DIL[h // 2]
        t = np.where(np.abs(delta) <= 64, -slopes[h] * (d * np.abs(delta)).astype(np.float32), np.float32(NEG))
        tabs[:, h, :] = t.reshape(128, 384)
    return tabs


_NC_CACHE = {}


def kernel(x, mem, mem_norm_g, g_pre_mix, w_in, rpb_na, w_mem_kv, b_gate, w_br_a, w_br_b, w_br_m, w_out,
           g_post_mix, g_pre_ffn, w_up, conv_w, conv_b, w_down, g_post_ffn):
    f = lambda a: np.ascontiguousarray(np.asarray(a, np.float32))
    x = f(x)
    mem = f(mem)
    B = x.shape[0]
    assert B * 2 == N_CORES
    per_hf = []
    for hf in range(2):
        cols = [_colvec(mem_norm_g)]
        for l in range(2):
            cols += [_colvec(g_pre_mix[l]), _colvec(g_post_mix[l]), _colvec(g_pre_ffn[l]), _colvec(g_post_ffn[l])]
            cols += [_colvec(np.asarray(b_gate[l]).reshape(-1))]
            cw = np.asarray(conv_w[l], np.float32)
            if hf:
                cw = cw[::-1]
            cols += [_colvec(np.ascontiguousarray(cw).reshape(-1))]
            cols += [_colvec(conv_b[l])]
        cv = np.ascontiguousarray(np.concatenate(cols, axis=1))
        assert cv.shape == (128, NCV), cv.shape
        nab = np.stack([_na_tables(np.asarray(rpb_na[l], np.float32), bool(hf)) for l in range(2)])
        per_hf.append((cv, nab))
    shared = {"dilb": _dil_tables(), "ident": np.eye(128, dtype=np.float32), "w_in": f(w_in), "w_mem_kv": f(w_mem_kv), "w_br_a": f(w_br_a),
              "w_br_b": f(w_br_b), "w_br_m": f(w_br_m), "w_out": f(w_out), "w_up": f(w_up), "w_down": f(w_down)}
    in_maps = []
    for c in range(N_CORES):
        b, hf = c // 2, c % 2
        m = dict(shared)
        m["cv"], m["nab"] = per_hf[hf]
        xs = x[b][::-1] if hf else x[b]
        m["xT"] = np.ascontiguousarray(xs.reshape(NB, 512, 8, 128).transpose(0, 3, 2, 1))
        m["memT"] = np.ascontiguousarray(mem[b].T)
        in_maps.append(m)
    if "nc" not in _NC_CACHE:
        _NC_CACHE["nc"] = build_nc()
    res = run_bass_kernel_spmd(_NC_CACHE["nc"], in_maps, core_ids=list(range(N_CORES)))
    out = np.empty((B, SEQ, D), np.float32)
    for c in range(N_CORES):
        b, hf = c // 2, c % 2
        o = res.results[c]["outT"].transpose(0, 3, 2, 1).reshape(OWN, D)
        if hf:
            out[b, SEQ - OWN:] = o[::-1]
        else:
            out[b, :OWN] = o
    return out
```
